# Optimizing a Trainium2 kernel written in Bass

```python
import math
import jax, jax.numpy as jnp
from jax import lax
import numpy as np

D_MODEL = 4096
BATCH = 4
SEQ = 4096
DEPTH = 2

D_CONV = D_MODEL // 4
CONV_W = 3
N_HEADS = 16
HEAD_DIM = 128
D_ATT = N_HEADS * HEAD_DIM
D_LAT = 256
IDX_HEADS = 32
IDX_DIM = 64
TOPK_MAX = 256
Q_BLOCK = 128
POOL_WINDOWS = (2, 4, 8, 16)
N_POOL_GROUPS = 4
D_POOL = D_MODEL // 4
POOL_GROUP = D_POOL // N_POOL_GROUPS
N_BRANCH = 3
D_FF = ((8 * D_MODEL // 3 + 255) // 256) * 256
ALPHA = (2.0 * DEPTH) ** 0.25
BETA = (8.0 * DEPTH) ** -0.25
LN_EPS = 1e-5
RMS_EPS = 1e-6
IN_SPLITS = (D_CONV, D_CONV, D_CONV,
             D_ATT, D_LAT,
             IDX_HEADS * IDX_DIM, IDX_DIM, IDX_HEADS,
             D_POOL,
             N_BRANCH * D_MODEL)
N_IN = sum(IN_SPLITS)

kernel_name = "hybrid_conv_dsa_pool_deepnorm"


def layer_norm(x, g, b):
    xf = x.astype(jnp.float32)
    mu = jnp.mean(xf, axis=-1, keepdims=True)
    var = jnp.mean(jnp.square(xf - mu), axis=-1, keepdims=True)
    return ((xf - mu) * lax.rsqrt(var + LN_EPS) * g + b).astype(x.dtype)


def rms_norm(x, g):
    xf = x.astype(jnp.float32)
    return (xf * lax.rsqrt(jnp.mean(xf * xf, axis=-1, keepdims=True) + RMS_EPS) * g).astype(x.dtype)


def causal_dwconv(u, w):
    k = w.shape[0]
    s = u.shape[1]
    up = jnp.pad(u, ((0, 0), (k - 1, 0), (0, 0)))
    out = up[:, 0:s] * w[0]
    for j in range(1, k):
        out = out + up[:, j:j + s] * w[j]
    return out


def short_conv_mixer(b_gate, c_gate, v, conv_w):
    return b_gate * causal_dwconv(c_gate * v, conv_w)


def pool_mixer(u, pool_w, pool_scale):
    bsz, s, _ = u.shape
    uf = u.astype(jnp.float32)
    c0 = jnp.concatenate([jnp.zeros((bsz, 1, D_POOL), jnp.float32), jnp.cumsum(uf, axis=1)], axis=1)
    t1 = jnp.arange(1, s + 1, dtype=jnp.float32)[:, None]
    outs = []
    for g, w in enumerate(POOL_WINDOWS):
        sl = slice(g * POOL_GROUP, (g + 1) * POOL_GROUP)
        cg = c0[:, :, sl]
        prev = jnp.concatenate([jnp.zeros((bsz, w - 1, POOL_GROUP), jnp.float32), cg[:, :s - w + 1]], axis=1)
        mean = (cg[:, 1:] - prev) / jnp.minimum(t1, float(w))
        outs.append(mean - uf[:, :, sl])
    d = jnp.stack(outs, axis=2).astype(u.dtype)
    y = jnp.einsum('bsgc,gcd->bsgd', d, pool_w).reshape(bsz, s, D_POOL)
    return y * pool_scale


def dsa_attention(q, c_kv, q_idx, k_idx, w_idx, w_uk, w_uv):
    bsz, s = q.shape[0], q.shape[1]
    k_sel = min(TOPK_MAX, s // 4)
    nb = s // Q_BLOCK
    key_pos = jnp.arange(s)
    k_idx_f = k_idx.astype(jnp.float32)

    def to_blocks(a):
        return a.reshape((bsz, nb, Q_BLOCK) + a.shape[2:]).swapaxes(0, 1)

    def block(args):
        blk, qb, qib, wb = args
        tpos = blk * Q_BLOCK + jnp.arange(Q_BLOCK)
        causal = key_pos[None, :] <= tpos[:, None]
        logits = jnp.einsum('bthd,bsd->bths', qib.astype(jnp.float32), k_idx_f) * (IDX_DIM ** -0.5)
        score = jnp.einsum('bth,bths->bts', wb.astype(jnp.float32), jax.nn.relu(logits)) * (IDX_HEADS ** -0.5)
        score = jnp.where(causal[None], score, -jnp.inf)
        _, sel = lax.top_k(score, k_sel)
        ckv_sel = jax.vmap(lambda c, i: c[i])(c_kv, sel)
        valid = sel <= tpos[None, :, None]
        q_lat = jnp.einsum('bthd,hdc->bthc', qb, w_uk)
        sc = jnp.einsum('bthc,btkc->bthk', q_lat, ckv_sel).astype(jnp.float32) * (HEAD_DIM ** -0.5)
        sc = jnp.where(valid[:, :, None, :], sc, -jnp.inf)
        p = jax.nn.softmax(sc, axis=-1).astype(qb.dtype)
        o_lat = jnp.einsum('bthk,btkc->bthc', p, ckv_sel)
        return jnp.einsum('bthc,hcd->bthd', o_lat, w_uv)

    out = lax.map(block, (jnp.arange(nb), to_blocks(q), to_blocks(q_idx), to_blocks(w_idx)))
    return out.swapaxes(0, 1).reshape(bsz, s, D_ATT)


def hybrid_mixer(x, w_in, b_gate, conv_a, kv_norm, w_uk, w_uv, pool_w, pool_scale,
                 w_br_a, w_br_b, w_br_c, w_o):
    bsz, s, _ = x.shape
    z = x @ w_in
    splits = np.cumsum(IN_SPLITS)[:-1].tolist()
    bg, cg, v, q, ckv, qi, ki, wi, u_pool, gate_logits = jnp.split(z, splits, axis=-1)
    y_a = short_conv_mixer(bg, cg, v, conv_a)
    y_b = dsa_attention(q.reshape(bsz, s, N_HEADS, HEAD_DIM), rms_norm(ckv, kv_norm),
                        qi.reshape(bsz, s, IDX_HEADS, IDX_DIM), ki, wi, w_uk, w_uv)
    y_c = pool_mixer(u_pool, pool_w, pool_scale)
    g = jax.nn.sigmoid(gate_logits.reshape(bsz, s, N_BRANCH, D_MODEL) + b_gate)
    merged = (g[:, :, 0] * (y_a @ w_br_a)
              + g[:, :, 1] * (y_b @ w_br_b)
              + g[:, :, 2] * (y_c @ w_br_c))
    return merged @ w_o


def conv_ffn(x, w_up, w_conv, w_down):
    gt, up = jnp.split(x @ w_up, 2, axis=-1)
    h = jax.nn.silu(causal_dwconv(gt, w_conv)) * up
    return h @ w_down


def setup_inputs(seed: int = 0) -> dict:
    key = jax.random.key(seed)
    ks = jax.random.split(key, 24)

    def nrm(k, shape, scale):
        return jax.random.normal(k, shape, jnp.float32) * scale

    L = DEPTH
    return {
        "x": nrm(ks[0], (BATCH, SEQ, D_MODEL), 1.0),
        "w_in": nrm(ks[1], (L, D_MODEL, N_IN), D_MODEL ** -0.5),
        "b_gate": nrm(ks[2], (L, N_BRANCH, D_MODEL), 0.02),
        "conv_a": nrm(ks[3], (L, CONV_W, D_CONV), CONV_W ** -0.5),
        "kv_norm": 1.0 + nrm(ks[4], (L, D_LAT), 0.02),
        "w_uk": nrm(ks[5], (L, N_HEADS, HEAD_DIM, D_LAT), HEAD_DIM ** -0.5),
        "w_uv": nrm(ks[6], (L, N_HEADS, D_LAT, HEAD_DIM), D_LAT ** -0.5),
        "pool_w": nrm(ks[7], (L, N_POOL_GROUPS, POOL_GROUP, POOL_GROUP), POOL_GROUP ** -0.5),
        "pool_scale": 1.0 + nrm(ks[8], (L, D_POOL), 0.02),
        "w_br_a": nrm(ks[9], (L, D_CONV, D_MODEL), BETA * D_CONV ** -0.5),
        "w_br_b": nrm(ks[10], (L, D_ATT, D_MODEL), BETA * D_ATT ** -0.5),
        "w_br_c": nrm(ks[11], (L, D_POOL, D_MODEL), BETA * D_POOL ** -0.5),
        "w_o": nrm(ks[12], (L, D_MODEL, D_MODEL), BETA * D_MODEL ** -0.5),
        "ln1_g": 1.0 + nrm(ks[13], (L, D_MODEL), 0.02),
        "ln1_b": nrm(ks[14], (L, D_MODEL), 0.02),
        "w_up": nrm(ks[15], (L, D_MODEL, 2 * D_FF), D_MODEL ** -0.5),
        "conv_ffn_w": nrm(ks[16], (L, CONV_W, D_FF), CONV_W ** -0.5),
        "w_down": nrm(ks[17], (L, D_FF, D_MODEL), BETA * D_FF ** -0.5),
        "ln2_g": 1.0 + nrm(ks[18], (L, D_MODEL), 0.02),
        "ln2_b": nrm(ks[19], (L, D_MODEL), 0.02),
    }


def reference(x, w_in, b_gate, conv_a, kv_norm, w_uk, w_uv, pool_w, pool_scale,
              w_br_a, w_br_b, w_br_c, w_o, ln1_g, ln1_b, w_up, conv_ffn_w, w_down,
              ln2_g, ln2_b):
    for l in range(DEPTH):
        mix = hybrid_mixer(x, w_in[l], b_gate[l], conv_a[l], kv_norm[l], w_uk[l], w_uv[l],
                           pool_w[l], pool_scale[l], w_br_a[l], w_br_b[l], w_br_c[l], w_o[l])
        x = layer_norm(ALPHA * x + mix, ln1_g[l], ln1_b[l])
        ff = conv_ffn(x, w_up[l], conv_ffn_w[l], w_down[l])
        x = layer_norm(ALPHA * x + ff, ln2_g[l], ln2_b[l])
    return x
```

```python
import numpy as np
from contextlib import ExitStack
import concourse.bass as bass
import concourse.mybir as mybir
from concourse.bass_utils import run_bass_kernel_spmd

F32 = mybir.dt.float32
BF16 = mybir.dt.bfloat16
AF = mybir.ActivationFunctionType
ALU = mybir.AluOpType

D = 4096
KC = D // 128
D_CONV = 1024
D_ATT = 2048
NH = 16
D_LAT = 256
IDXH = 32
IDXD = 64
D_POOL = 1024
D_FF = 11008
NFF = D_FF // 128
N_IN = 20832
DEPTH = 2
ALPHA = (2.0 * DEPTH) ** 0.25
LN_EPS = 1e-5
RMS_EPS = 1e-6
TOPK = 256
NEG = -1.0e30
NCORES = 8

SEG = {}
_o = 0
for _n, _w in (("bg", 1024), ("cg", 1024), ("v", 1024), ("q", 2048), ("ckv", 256), ("qi", 2048),
               ("ki", 64), ("wi", 32), ("up", 1024), ("g", 12288)):
    SEG[_n] = (_o, _w)
    _o += _w
assert _o == N_IN

V_BG = 0
V_CA = 96
V_KVN = 120
V_PS = 122
V_L1G = 130
V_L1B = 162
V_CF = 194
V_L2G = 452
V_L2B = 484
NV = 516
C_H = 0
C_NEGB = 1
C_PC = 2
NCST = 66


class Sem:
    def __init__(self, h, bg=False):
        self.h = h
        self.cnt = 0
        self.bg = bg


class Prog:
    ENG = ("sync", "scalar", "vector", "gpsimd", "tensor")

    def __init__(self, nc, es):
        self.nc = nc
        self.es = es
        self.q = {e: [] for e in self.ENG}
        self.sems = []
        self.semcache = {}
        self.tick = {e: self.newsem("tk_" + e) for e in ("scalar", "vector", "gpsimd", "tensor")}
        self.pending = {e: [] for e in self.ENG}

    def newsem(self, name, bg=False):
        if name in self.semcache:
            return self.semcache[name]
        s = Sem(self.es.enter_context(self.nc.semaphore(name)), bg)
        self.sems.append(s)
        self.semcache[name] = s
        return s

    def _waits(self, eng, waits):
        w = []

        def fl(x):
            if x is None:
                return
            if isinstance(x, tuple) and len(x) == 2 and isinstance(x[0], Sem):
                w.append(x)
            else:
                for y in x:
                    fl(y)
        fl(waits)
        if self.pending[eng]:
            w = self.pending[eng] + w
            self.pending[eng] = []
        return tuple(w)

    def op(self, eng, fn, waits=(), sig=True):
        tok = None
        inc = None
        w = self._waits(eng, waits)
        if sig:
            s = self.tick[eng]
            s.cnt += 1
            tok = (s, s.cnt)
            inc = (s, 1)
        self.q[eng].append((w, fn, inc))
        return tok

    def dma(self, eng, out, in_, sem, waits=(), **kw):
        w = self._waits(eng, waits)
        sem.cnt += 16
        self.q[eng].append((w, lambda e: e.dma_start(out=out, in_=in_, **kw), (sem, 16)))
        return (sem, sem.cnt)

    def coll(self, groups, in_ap, out_ap, sem, waits=()):
        w = self._waits("gpsimd", waits)
        sem.cnt += 1
        self.q["gpsimd"].append((w, lambda e: e.collective_compute(
            "AllGather", ALU.bypass, replica_groups=groups, ins=[in_ap], outs=[out_ap]), (sem, 1)))
        return (sem, sem.cnt)

    def fence(self):
        toks = [(s, s.cnt) for s in self.sems if (not s.bg) and s.cnt > 0]
        for e in self.ENG:
            self.pending[e] = list(toks)

    def alltoks(self):
        return [(s, s.cnt) for s in self.sems if s.cnt > 0]

    def emit(self, block, final_waits):
        for eng in self.ENG:
            ops = self.q[eng]
            extra = final_waits if eng == "sync" else ()

            def body(e, ops=ops, extra=extra):
                seen = {}
                for waits, fn, inc in ops:
                    for (s, v) in waits:
                        if seen.get(id(s), 0) < v:
                            e.wait_ge(s.h, v)
                            seen[id(s)] = v
                    ins = fn(e)
                    if inc is not None:
                        ins.then_inc(inc[0].h, inc[1])
                for (s, v) in extra:
                    if seen.get(id(s), 0) < v:
                        e.wait_ge(s.h, v)
                        seen[id(s)] = v
            getattr(block, eng)(body)


class Arena:
    def __init__(self, ap, nbytes):
        self.ap = ap
        self.nbytes = nbytes
        self.off = 0

    def mark(self):
        return self.off

    def release(self, m):
        self.off = m

    def alloc(self, shape_free, dtype, parts=128):
        esz = 4 if dtype == F32 else 2
        n = 1
        for s in shape_free:
            n *= s
        nb = (n * esz + 63) // 64 * 64
        assert self.off + nb <= self.nbytes, ("SBUF arena overflow", self.off, nb, self.nbytes)
        a = self.ap[0:parts, self.off // 2:(self.off + n * esz) // 2]
        self.off += nb
        if dtype == F32:
            a = a.bitcast(F32)
        if len(shape_free) == 2:
            a = a.rearrange("p (a b) -> p a b", b=shape_free[1])
        elif len(shape_free) == 3:
            a = a.rearrange("p (a b c) -> p a b c", b=shape_free[1], c=shape_free[2])
        return a


ARENA_BYTES = 184 * 1024


def build(T, depth, upto=None, dbg=(), need=None):
    NTC = T // 512
    NQ = T // 128
    ALLWN = ("w_in", "w_br_a", "w_br_b", "w_br_c", "w_o", "w_up", "w_down")
    need = set(ALLWN if need is None else need)
    nc = bass.Bass("TRN2", target_bir_lowering=False)
    es = ExitStack()
    P = Prog(nc, es)

    def din(name, shape, dt=F32):
        return nc.dram_tensor(name, list(shape), dt, kind="ExternalInput").ap()

    def dscr(name, shape, dt):
        return nc.dram_tensor(name, list(shape), dt, kind="Internal").ap()

    x_in = din("x", [T, D])
    wmeta = {"w_in": (D, N_IN), "w_br_a": (D_CONV, D), "w_br_b": (D_ATT, D), "w_br_c": (D_POOL, D),
             "w_o": (D, D), "w_up": (D, 2 * D_FF), "w_down": (D_FF, D)}
    wsh = {nme: din(nme, [depth, R // 2, C]) for nme, (R, C) in wmeta.items() if nme in need}
    w_uk_in = din("w_uk", [depth, NH, 128, D_LAT])
    w_uv_in = din("w_uv", [depth, NH, D_LAT, 128])
    pool_w_in = din("pool_w", [depth, 4, 256, 256])
    vecs_in = din("vecs", [depth, 128, NV])
    cst_in = din("cst", [128, NCST])
    y_out = nc.dram_tensor("y", [T, D], F32, kind="ExternalOutput").ap()
    dbg_out = {}

    WGROUPS = (("w_in",), ("w_up",), ("w_down", "w_o", "w_br_a", "w_br_b", "w_br_c"))
    gsh = {}
    woff = {}
    for l in range(depth):
        for gi, grp in enumerate(WGROUPS):
            off = 0
            for nme in grp:
                if nme in need:
                    woff[nme] = (gi, off)
                    off += (wmeta[nme][0] // 2) * wmeta[nme][1]
            if off:
                gsh[(gi, l)] = nc.dram_tensor(f"wg{gi}_{l}", [2, off], BF16, kind="Internal", addr_space="Shared").ap()

    def wrows(nme, l, ra, rb):
        R, C = wmeta[nme]
        R2 = R // 2
        p = ra // R2
        assert (rb - 1) // R2 == p, (nme, ra, rb)
        gi, off = woff[nme]
        return gsh[(gi, l)][p, off + (ra - p * R2) * C:off + (rb - p * R2) * C].rearrange("(r c) -> r c", c=C)

    wfull = {(nme, l): (nme, l) for l in range(depth) for nme in wmeta if nme in need}
    xT32 = dscr("xT32", [D, T], F32)
    xTb = dscr("xTb", [D, T], BF16)
    x1T32 = dscr("x1T32", [D, T], F32)
    x1Tb = dscr("x1Tb", [D, T], BF16)
    vT = dscr("vT", [D, T], F32)
    z_conv = dscr("z_conv", [3072, T], F32)
    z_q = dscr("z_q", [2048, T], BF16)
    z_ckv = dscr("z_ckv", [256, T], F32)
    z_qi = dscr("z_qi", [2048, T], BF16)
    z_ki = dscr("z_ki", [64, T], BF16)
    z_wi = dscr("z_wi", [32, T], F32)
    z_up = dscr("z_up", [1024, T], F32)
    z_g = dscr("z_g", [12288, T], BF16)
    yT = dscr("yT", [D, T], BF16)
    hT = dscr("hT", [D_FF, T], BF16)
    ex1s = dscr("ex1s", [320, T], BF16)
    ex1sp = [dscr("ex1s_a", [128, T], BF16), dscr("ex1s_b", [128, T], BF16), dscr("ex1s_c", [64, T], BF16)]
    ex1rp = [dscr("ex1r_a", [256, T], BF16), dscr("ex1r_b", [256, T], BF16), dscr("ex1r_c", [128, T], BF16)]
    ex1r = dscr("ex1r", [640, T], BF16)
    ex1hs = dscr("ex1hs", [3072, 16], F32)
    ex1hr = dscr("ex1hr", [6144, 16], F32)
    ex2s = dscr("ex2s", [D, 16], BF16)
    ex2r = dscr("ex2r", [2 * D, 16], BF16)
    wukb = [dscr(f"wukb{l}", [NH, 128, D_LAT], BF16) for l in range(depth)]
    wuvb = [dscr(f"wuvb{l}", [NH, D_LAT, 128], BF16) for l in range(depth)]
    poolwb = [dscr(f"poolwb{l}", [4, 256, 256], BF16) for l in range(depth)]

    arena_t = es.enter_context(nc.sbuf_tensor("arena", [128, ARENA_BYTES // 2], BF16))
    A = Arena(arena_t[:], ARENA_BYTES)
    ps = [es.enter_context(nc.psum_tensor(f"ps{i}", [128, 512], F32)) for i in range(8)]

    identf = A.alloc([128], F32)
    identb = A.alloc([128], BF16)
    onesf = A.alloc([128], F32)
    onesb = A.alloc([128], BF16)
    tri = A.alloc([128], F32)
    cst = A.alloc([NCST], F32)
    vecs = [A.alloc([NV], F32) for _ in range(depth)]
    ffhalo = A.alloc([NFF, 2], F32)

    s_const = P.newsem("m2")
    t = P.dma("sync", cst, cst_in[:, :], s_const)
    for l in range(depth):
        t = P.dma("sync", vecs[l], vecs_in[l], s_const)
    t = P.op("gpsimd", lambda e: e.memset(identf, 0.0))
    t = P.op("gpsimd", lambda e: e.affine_select(out=identf, in_=identf, pattern=[[-1, 128]], compare_op=ALU.not_equal,
                                                  fill=1.0, base=0, channel_multiplier=1), waits=[t])
    t = P.op("gpsimd", lambda e: e.tensor_copy(identb, identf), waits=[t])
    P.op("gpsimd", lambda e: e.memset(onesf, 1.0))
    P.op("gpsimd", lambda e: e.memset(onesb, 1.0))
    t = P.op("gpsimd", lambda e: e.memset(tri, 0.0))
    t = P.op("gpsimd", lambda e: e.affine_select(out=tri, in_=tri, pattern=[[-1, 128]], compare_op=ALU.is_ge,
                                                  fill=NEG, base=0, channel_multiplier=1), waits=[t])
    P.op("gpsimd", lambda e: e.memset(ffhalo, 0.0))

    QUAD = [[0, 1, 2, 3], [4, 5, 6, 7]]
    PAIR4 = [[0, 4], [1, 5], [2, 6], [3, 7]]
    PAIR = [[0, 1], [2, 3], [4, 5], [6, 7]]
    wready = {}

    pidc = {}

    def parity(e):
        if id(e) not in pidc:
            pidc[id(e)] = e.partition_id() % 2
        return pidc[id(e)]

    bar_in = dscr("bar_in", [2, 64], F32)
    bar_out = dscr("bar_out", [4, 64], F32)

    def prep_cast(names, l):
        for nme in names:
            if nme not in need:
                continue
            sm = P.newsem(f"wc_{nme}_{l}", bg=True)
            R2 = wmeta[nme][0] // 2
            src = wsh[nme]
            gi_, off_ = woff[nme]
            dstt = gsh[(gi_, l)]
            Cc = wmeta[nme][1]
            tk = None
            stepc = R2 // 2 if nme in ("w_in", "w_up") else R2
            for r0 in range(0, R2, stepc):
                r1 = min(R2, r0 + stepc)
                w = P._waits("gpsimd", ())
                sm.cnt += 16
                P.q["gpsimd"].append((w, (lambda e, r0=r0, r1=r1, src=src, dstt=dstt, off_=off_, Cc=Cc, R2=R2: e.dma_start(
                    out=dstt[:, off_:off_ + R2 * Cc].rearrange("two (r c) -> two r c", c=Cc)[bass.ds(parity(e), 1), r0:r1, :],
                    in_=src[l:l + 1, r0:r1, :])), (sm, 16)))
                tk = (sm, sm.cnt)
            castt[(nme, l)] = tk

    def prep_bar(names, l):
        for nme in names:
            if nme not in need:
                continue
            sm = P.newsem(f"wb_{nme}_{l}", bg=True)
            wready[(nme, l)] = P.coll(PAIR, bar_in[:, :], bar_out[:, :], sm, waits=[castt[(nme, l)]])

    castt = {}

    stage_cnt = [0]

    class Stage:
        def __init__(self, n, name):
            self.n = n
            self.buf = [A.alloc([512], F32) for _ in range(n)]
            self.sem = [P.newsem(f"st_{name}_{i}") for i in range(n)]
            self.last = [None] * n
            self.i = 0

        def get(self):
            k = self.i % self.n
            self.i += 1
            return k

    def bfview(ap_f32_512):
        return ap_f32_512.bitcast(BF16)[:, 0:512]

    evac_rr = [0]

    def evac_engine():
        import os
        evac_rr[0] += 1
        if os.environ.get("EVAC"):
            return os.environ["EVAC"]
        return "scalar" if (evac_rr[0] % 2) else "vector"

    def copy_op(eng, out, in_, waits):
        if eng == "scalar":
            return P.op("scalar", lambda e: e.activation(out=out, in_=in_, func=AF.Identity), waits=waits)
        return P.op(eng, lambda e: e.tensor_copy(out, in_), waits=waits)

    class WStream:
        def __init__(self, nslots, slot_elems):
            self.n = nslots
            self.slots = [A.alloc([slot_elems], BF16) for _ in range(nslots)]
            self.sems = [P.newsem(f"wslot{i}") for i in range(nslots)]
            self.free_tok = [None] * nslots
            self.i = 0

        def load(self, Wd, k0c, kcn, c0, bw, ready_tok, eng="scalar"):
            k = self.i % self.n
            self.i += 1
            slot = self.slots[k][:, 0:kcn * bw].rearrange("p (c n) -> p c n", n=bw)
            nme, l = Wd
            half_ch = wmeta[nme][0] // 256
            tok = None
            a = 0
            while a < kcn:
                ga = k0c + a
                lim = ((ga // half_ch) + 1) * half_ch - k0c
                b = min(kcn, a + 8, lim)
                src = wrows(nme, l, ga * 128, (k0c + b) * 128)[:, c0:c0 + bw].rearrange("(c p) n -> p c n", p=128)
                tok = P.dma(eng, slot[:, a:b, :], src, self.sems[k], waits=[self.free_tok[k], ready_tok])
                a = b
            return k, slot, tok

    psrr = [0]
    ps_free = [None] * 8

    def gemm(Wd, ready_tok, A_sb, a_tok, k0c, kcn, blocks, ws, epilogue, psbanks=(0, 1, 2, 3), dma_eng="scalar", extra=None):
        nb = len(blocks)
        loaded = {}

        def issue(bi):
            cols = blocks[bi]
            c0 = cols[0][0]
            bw = cols[-1][0] + cols[-1][1] - c0
            loaded[bi] = ws.load(Wd, k0c, kcn, c0, bw, ready_tok, eng=dma_eng)

        pre = ws.n - 1
        for bi in range(min(pre, nb)):
            issue(bi)
        for bi in range(nb):
            k, slot, wtok = loaded.pop(bi)
            cols = blocks[bi]
            c0 = cols[0][0]
            last_mm = None
            for (cc0, w, tag) in cols:
                pb = psbanks[psrr[0] % len(psbanks)]
                psrr[0] += 1
                pst = ps[pb]
                o = cc0 - c0
                for kc in range(kcn):
                    waits = []
                    if kc == 0:
                        waits = [wtok, a_tok, ps_free[pb]]
                    lhsT = slot[:, kc, o:o + w]
                    rhs = A_sb[:, kc, :]
                    st, sp = (kc == 0), (kc == kcn - 1)
                    mm = P.op("tensor", lambda e, lhsT=lhsT, rhs=rhs, st=st, sp=sp, pst=pst, w=w:
                              e.matmul(pst[0:w, :], lhsT, rhs, start=st, stop=sp), waits=waits, sig=sp)
                last_mm = mm
                if extra is not None and tag[0] == "gt":
                    xh, htok = extra
                    xm = None
                    for kc in range(kcn):
                        lhsT = slot[:, kc, o:o + w]
                        xm = P.op("tensor", lambda e, lhsT=lhsT, kc=kc: e.matmul(
                            ps[4][:, 0:16], lhsT, xh[:, kc, :], start=(kc == 0), stop=(kc == kcn - 1)),
                            waits=[htok, ps_free[4]] if kc == 0 else [], sig=(kc == kcn - 1))
                    last_mm = xm
                    ps_free[pb] = epilogue(pst, w, tag, mm, xm)
                else:
                    ps_free[pb] = epilogue(pst, w, tag, mm)
            ws.free_tok[k] = last_mm
            if bi + pre < nb:
                issue(bi + pre)

    def mkblocks(chunks, per=4):
        out = []
        cur = []
        for c in chunks:
            if cur and (len(cur) >= per or cur[-1][0] + cur[-1][1] != c[0]):
                out.append(cur)
                cur = []
            cur.append(c)
        if cur:
            out.append(cur)
        return out

    def tcols(tc):
        return slice(tc * 512, (tc + 1) * 512)

    def load_A(dst, src2d, kcn, sem, waits=None, eng="sync"):
        v = src2d.rearrange("(c p) t -> p c t", p=128)
        tok = None
        step = 8
        for a in range(0, kcn, step):
            b = min(kcn, a + step)
            tok = P.dma(eng, dst[:, a:b, :], v[:, a:b, :], sem, waits=waits)
        return tok

    def phase_transpose_in():
        import os
        stg = int(os.environ.get("TIN_STAGE", "9"))
        m = A.mark()
        xin = [A.alloc([D], F32) for _ in range(2)]
        s_xin = [P.newsem(f"pp_ld{i}") for i in range(2)]
        st = Stage(6, "A")
        xin_free = [None, None]
        for ti in range(NQ):
            b = ti % 2
            ld = P.dma("sync", xin[b], x_in[ti * 128:(ti + 1) * 128, :], s_xin[b], waits=[xin_free[b]])
            mm = None
            if stg < 2:
                continue
            for g4 in range(KC // 4):
                pb = 4 + (g4 % 4)
                for j in range(4):
                    c = g4 * 4 + j
                    mm = P.op("tensor", lambda e, c=c, j=j, pb=pb, b=b: e.matmul(
                        ps[pb][:, j * 128:(j + 1) * 128], xin[b][:, c * 128:(c + 1) * 128], identf,
                        start=True, stop=True), waits=([ld, ps_free[pb]] if j == 0 else []), sig=(j == 3))
                if stg < 3:
                    continue
                k = st.get()
                eng_g = evac_engine()
                c1 = copy_op(eng_g, st.buf[k], ps[pb][:, :], [mm, st.last[k]])
                if stg >= 4 and stg != 5:
                    st.last[k] = P.dma("sync", xT32[g4 * 512:(g4 + 1) * 512, ti * 128:(ti + 1) * 128].rearrange("(c p) t -> p c t", p=128),
                                       st.buf[k].rearrange("p (c t) -> p c t", t=128), st.sem[k], waits=[c1])
                k2 = st.get()
                b2 = bfview(st.buf[k2])
                c2 = copy_op(eng_g, b2, ps[pb][:, :], [mm, st.last[k2]])
                if stg >= 5:
                    st.last[k2] = P.dma(os.environ.get("Q2", "sync"), xTb[g4 * 512:(g4 + 1) * 512, ti * 128:(ti + 1) * 128].rearrange("(c p) t -> p c t", p=128),
                                        b2.rearrange("p (c t) -> p c t", t=128), st.sem[k2], waits=[c2])
                ps_free[pb] = [c1, c2]
            xin_free[b] = mm
        A.release(m)

    def inproj_chunks():
        ch = []
        for seg in ("bg", "cg", "v", "q", "ckv", "qi", "ki", "wi", "up", "g"):
            s0, w = SEG[seg]
            i = 0
            while i * 128 < w:
                ww = min(128, w - i * 128)
                ch.append((s0 + i * 128, ww, (seg, i)))
                i += 1
        return ch

    def phase_inproj(l, tc, only=None):
        m = A.mark()
        Ax = A.alloc([KC, 512], BF16)
        s_a = P.newsem("a_ld")
        a_tok = load_A(Ax, xTb[:, tcols(tc)], KC, s_a)
        ws = WStream(3, KC * 512)
        st = Stage(6, "A")
        vv = vecs[l]

        def epi(pst, w, tag, mm):
            seg, idx = tag
            k = st.get()
            rows = slice(idx * 128, idx * 128 + w)
            if seg == "g":
                o = bfview(st.buf[k])
                c = P.op("scalar", lambda e: e.activation(out=o[0:w, :], in_=pst[0:w, :], func=AF.Sigmoid,
                                                          bias=vv[0:w, V_BG + idx:V_BG + idx + 1], scale=1.0),
                         waits=[mm, st.last[k]])
                dst = z_g[rows, tcols(tc)]
            elif seg in ("q", "qi", "ki"):
                o = bfview(st.buf[k])
                c = copy_op(evac_engine(), o[0:w, :], pst[0:w, :], [mm, st.last[k]])
                dst = {"q": z_q, "qi": z_qi, "ki": z_ki}[seg][rows, tcols(tc)]
            else:
                o = st.buf[k]
                c = copy_op(evac_engine(), o[0:w, :], pst[0:w, :], [mm, st.last[k]])
                if seg in ("bg", "cg", "v"):
                    base = {"bg": 0, "cg": 1024, "v": 2048}[seg]
                    dst = z_conv[base + idx * 128:base + idx * 128 + w, tcols(tc)]
                else:
                    dst = {"ckv": z_ckv, "wi": z_wi, "up": z_up}[seg][rows, tcols(tc)]
            st.last[k] = P.dma("sync", dst, o[0:w, :], st.sem[k], waits=[c])
            return c

        ch = inproj_chunks()
        if only is not None:
            ch = [c for c in ch if c[2][0] in only]
        gemm(wfull[("w_in", l)], wready[("w_in", l)], Ax, a_tok, 0, KC, mkblocks(ch), ws, epi)
        A.release(m)

    ex_sem = {}

    def phase_ex1(l):
        m = A.mark()
        ck = A.alloc([2, T], F32)
        sq = A.alloc([2, T], F32)
        cn = A.alloc([2, T], BF16)
        rs = [A.alloc([512], F32) for _ in range(2)]
        epsr = A.alloc([1], F32)
        epst = P.op("vector", lambda e: e.memset(epsr, RMS_EPS))
        s_l = P.newsem("m0")
        s_s = P.newsem("m1")
        ld = P.dma("sync", ck, z_ckv[:, :].rearrange("(c p) t -> p c t", p=128), s_l)
        sqt = P.op("scalar", lambda e: e.activation(out=sq, in_=ck, func=AF.Square), waits=[ld])
        vv = vecs[l]
        last = None
        for tc in range(NTC):
            pb = 4 + tc % 2
            mm = None
            for cc in range(2):
                mm = P.op("tensor", lambda e, cc=cc, pb=pb, tc=tc: e.matmul(ps[pb][:, :], onesf, sq[:, cc, tcols(tc)],
                                                                           start=(cc == 0), stop=(cc == 1)),
                          waits=[sqt, ps_free[pb]] if cc == 0 else [], sig=(cc == 1))
            r = rs[tc % 2]
            t1 = P.op("scalar", lambda e, r=r, pb=pb: e.activation(out=r, in_=ps[pb][:, :], func=AF.Sqrt, bias=epsr, scale=1.0 / D_LAT),
                      waits=[mm, last, epst])
            ps_free[pb] = t1
            t2 = P.op("vector", lambda e, r=r: e.reciprocal(r, r), waits=[t1])
            for cc in range(2):
                last = P.op("vector", lambda e, r=r, cc=cc, tc=tc: e.scalar_tensor_tensor(
                    cn[:, cc, tcols(tc)], ck[:, cc, tcols(tc)], vv[:, V_KVN + cc:V_KVN + cc + 1], r, ALU.mult, ALU.mult),
                    waits=[t2])
        d1 = P.dma("sync", ex1s[0:256, :].rearrange("(c p) t -> p c t", p=128), cn, s_s, waits=[last])
        d1a = P.dma("sync", ex1sp[0][:, :], cn[:, 0, :], s_s, waits=[last])
        d1b = P.dma("sync", ex1sp[1][:, :], cn[:, 1, :], s_s, waits=[last])
        d2 = P.dma("sync", ex1sp[2][:, :], z_ki[:, :], s_s)
        d3 = P.dma("sync", ex1hs[0:2048, 0:2], z_conv[1024:3072, T - 2:T], s_s)
        d4 = P.dma("sync", ex1hs[2048:3072, 0:15], z_up[:, T - 15:T], s_s)
        sc1 = P.newsem("ex1c", bg=True)
        g1 = P.coll(PAIR, ex1sp[0][:, :], ex1rp[0][:, :], sc1, waits=[d1, d1a, d1b, d2, d3, d4])
        g1 = P.coll(PAIR, ex1sp[1][:, :], ex1rp[1][:, :], sc1, waits=[g1])
        g1 = P.coll(PAIR, ex1sp[2][:, :], ex1rp[2][:, :], sc1, waits=[g1])
        g2 = P.coll(PAIR, ex1hs[:, :], ex1hr[:, :], sc1, waits=[g1])
        ex_sem[("ex1", l)] = g2
        A.release(m)

    def phase_convpool(l):
        m = A.mark()
        vv = vecs[l]
        ex = ex_sem[("ex1", l)]
        hflag = cst[:, C_H:C_H + 1]
        bufs = [[A.alloc([T + 16], F32) for _ in range(4)] for _ in range(2)]
        s_ld = [P.newsem(f"pp_ld{i}") for i in range(2)]
        s_st = [P.newsem(f"pp_st{i}") for i in range(2)]
        free = [None, None]
        for fc in range(8):
            b = fc % 2
            bg, cg, v, tmp = bufs[b]
            r = slice(fc * 128, (fc + 1) * 128)
            w8 = [free[b], ex]
            P.dma("sync", bg[:, 0:T], z_conv[r, :], s_ld[b], waits=w8)
            P.dma("sync", cg[:, 2:T + 2], z_conv[1024 + fc * 128:1024 + (fc + 1) * 128, :], s_ld[b], waits=w8)
            P.dma("sync", v[:, 2:T + 2], z_conv[2048 + fc * 128:2048 + (fc + 1) * 128, :], s_ld[b], waits=w8)
            P.dma("sync", cg[:, 0:2], ex1hr[fc * 128:(fc + 1) * 128, 0:2], s_ld[b], waits=w8)
            ld = P.dma("sync", v[:, 0:2], ex1hr[1024 + fc * 128:1024 + (fc + 1) * 128, 0:2], s_ld[b], waits=w8)
            o1 = P.op("vector", lambda e, cg=cg, v=v: e.tensor_tensor(cg[:, 0:T + 2], cg[:, 0:T + 2], v[:, 0:T + 2], ALU.mult), waits=[ld])
            o2 = P.op("vector", lambda e, cg=cg: e.tensor_scalar(cg[:, 0:2], cg[:, 0:2], hflag, None, ALU.mult), waits=[o1])
            w0 = vv[:, V_CA + 0 * 8 + fc:V_CA + 0 * 8 + fc + 1]
            w1 = vv[:, V_CA + 1 * 8 + fc:V_CA + 1 * 8 + fc + 1]
            w2 = vv[:, V_CA + 2 * 8 + fc:V_CA + 2 * 8 + fc + 1]
            o3 = P.op("vector", lambda e, cg=cg, tmp=tmp, w0=w0: e.tensor_scalar(tmp[:, 0:T], cg[:, 0:T], w0, None, ALU.mult), waits=[o2])
            o4 = P.op("vector", lambda e, cg=cg, tmp=tmp, w1=w1: e.scalar_tensor_tensor(tmp[:, 0:T], cg[:, 1:T + 1], w1, tmp[:, 0:T], ALU.mult, ALU.add), waits=[o3])
            o5 = P.op("vector", lambda e, cg=cg, tmp=tmp, w2=w2: e.scalar_tensor_tensor(tmp[:, 0:T], cg[:, 2:T + 2], w2, tmp[:, 0:T], ALU.mult, ALU.add), waits=[o4])
            yb = v.bitcast(BF16)[:, 0:T]
            o6 = P.op("vector", lambda e, bg=bg, tmp=tmp, yb=yb: e.tensor_tensor(yb, tmp[:, 0:T], bg[:, 0:T], ALU.mult), waits=[o5])
            free[b] = P.dma("sync", yT[r, :], yb, s_st[b], waits=[o6])
        dT = A.alloc([8, T], BF16)
        pw = A.alloc([4, 2, 256], BF16)
        s_pw = P.newsem("m0")
        pwt = P.dma("gpsimd", pw, pool_w_in[l].rearrange("g (kc k) n -> k g kc n", k=128), s_pw)
        free = [None, None]
        dlast = None
        for fc in range(8):
            b = fc % 2
            u, s_a, s_b, tmp = bufs[b]
            g = fc // 2
            wnd = 2 << g
            w8 = [free[b], ex]
            z0 = P.op("gpsimd", lambda e, u=u: e.memset(u[:, 0:1], 0.0), waits=w8)
            P.dma("sync", u[:, 16:T + 16], z_up[fc * 128:(fc + 1) * 128, :], s_ld[b], waits=w8)
            ld = P.dma("sync", u[:, 1:16], ex1hr[2048 + fc * 128:2048 + (fc + 1) * 128, 0:15], s_ld[b], waits=w8)
            o = P.op("vector", lambda e, u=u: e.tensor_scalar(u[:, 1:16], u[:, 1:16], hflag, None, ALU.mult), waits=[ld, z0])
            cur = u
            alt = [s_a, s_b]
            ai = 0
            k = 1
            while k < wnd:
                dst = alt[ai]
                ai ^= 1
                lo = 2 * k - 1
                o = P.op("vector", lambda e, cur=cur, dst=dst, lo=lo, k=k: e.tensor_tensor(
                    dst[:, lo:T + 16], cur[:, lo:T + 16], cur[:, lo - k:T + 16 - k], ALU.add), waits=[o])
                cur = dst
                k *= 2
            o = P.op("vector", lambda e, cur=cur, tmp=tmp, wnd=wnd: e.tensor_scalar(tmp[:, 0:T], cur[:, 16:T + 16], 1.0 / wnd, None, ALU.mult), waits=[o])
            o = P.op("vector", lambda e, tmp=tmp, g=g: e.tensor_tensor(tmp[:, 0:16], tmp[:, 0:16], cst[:, C_PC + g * 16:C_PC + (g + 1) * 16], ALU.mult), waits=[o])
            dlast = P.op("vector", lambda e, tmp=tmp, u=u, fc=fc: e.tensor_tensor(dT[:, fc, :], tmp[:, 0:T], u[:, 16:T + 16], ALU.subtract), waits=[o])
            free[b] = dlast
        st = Stage(4, "A")
        for tc in range(NTC):
            for g in range(4):
                for mo in range(2):
                    pb = psrr[0] % 4
                    psrr[0] += 1
                    mm = None
                    for kc in range(2):
                        mm = P.op("tensor", lambda e, g=g, mo=mo, kc=kc, pb=pb, tc=tc: e.matmul(
                            ps[pb][:, :], pw[:, g, kc, mo * 128:(mo + 1) * 128], dT[:, g * 2 + kc, tcols(tc)],
                            start=(kc == 0), stop=(kc == 1)), waits=[pwt, dlast, ps_free[pb]] if kc == 0 else [], sig=(kc == 1))
                    k = st.get()
                    o = bfview(st.buf[k])
                    fcn = g * 2 + mo
                    c = P.op("vector", lambda e, o=o, pb=pb, fcn=fcn: e.tensor_scalar(o, ps[pb][:, :], vv[:, V_PS + fcn:V_PS + fcn + 1], None, ALU.mult),
                             waits=[mm, st.last[k]])
                    ps_free[pb] = c
                    st.last[k] = P.dma("sync", yT[3072 + fcn * 128:3072 + (fcn + 1) * 128, tcols(tc)], o, st.sem[k], waits=[c])
        A.release(m)
    rfree = {}
    pfree = {}
    afree = [None, None]
    mfree = [None]
    ofree = [None]
    def phase_attn(l):
        m = A.mark()
        ex = ex_sem[("ex1", l)]
        NK = 2 * T
        kiT = A.alloc([NK], BF16, parts=64)
        ckvT = A.alloc([2, NK], BF16)
        ckvtok = A.alloc([NK // 128, 256], BF16)
        wuk = A.alloc([NH, 256], BF16)
        wuv = A.alloc([NH, 2, 128], BF16)
        acc = A.alloc([NK], F32)
        work = A.alloc([NK], F32)
        maskT = A.alloc([NK // 128, 128], BF16)
        qlat = A.alloc([2, NH, 128], BF16)
        qTb = [A.alloc([NH, 128], BF16) for _ in range(2)]
        qib = [A.alloc([IDXH, 128], BF16, parts=64) for _ in range(2)]
        wiTb = [A.alloc([128], F32, parts=32) for _ in range(2)]
        wtok = A.alloc([32], F32)
        m8 = A.alloc([256], F32)
        thr = A.alloc([1], F32)
        Rb = [A.alloc([512], F32) for _ in range(3)]
        pTb = [A.alloc([512], BF16) for _ in range(3)]
        rden = A.alloc([512], F32)
        onb = A.alloc([2, 512], BF16)
        ybst = [A.alloc([512], BF16) for _ in range(2)]
        s_k = P.newsem("m0")
        s_q = [P.newsem(f"pp_ld{i}") for i in range(2)]
        s_y = [P.newsem(f"pp_st{i}") for i in range(2)]
        lds = []
        lds.append(P.dma("sync", kiT[:, 0:T], ex1rp[2][0:64, :], s_k, waits=[ex]))
        lds.append(P.dma("sync", kiT[:, T:NK], z_ki[:, :], s_k))
        lds.append(P.dma("sync", ckvT[:, 0, 0:T], ex1rp[0][0:128, :], s_k, waits=[ex]))
        lds.append(P.dma("sync", ckvT[:, 1, 0:T], ex1rp[1][0:128, :], s_k, waits=[ex]))
        lds.append(P.dma("sync", ckvT[:, :, T:NK], ex1s[0:256, :].rearrange("(c p) t -> p c t", p=128), s_k))
        lds.append(P.dma("sync", wuk, wukb[l].rearrange("h d c -> d h c"), s_k))
        lds.append(P.dma("sync", wuv, wuvb[l].rearrange("h (cc c) d -> c h cc d", c=128), s_k))
        kld = lds[-1]
        lastck = None
        for j2 in range(NK // 256):
            pb = 4 + j2 % 2
            mm = None
            for jj in range(2):
                j = j2 * 2 + jj
                for cc in range(2):
                    mm = P.op("tensor", lambda e, j=j, jj=jj, cc=cc, pb=pb: e.matmul(
                        ps[pb][:, jj * 256 + cc * 128:jj * 256 + (cc + 1) * 128], ckvT[:, cc, j * 128:(j + 1) * 128], identb,
                        start=True, stop=True), waits=[lds, ps_free[pb]] if (jj == 0 and cc == 0) else [], sig=(jj == 1 and cc == 1))
            lastck = copy_op(evac_engine(), ckvtok[:, j2 * 2:j2 * 2 + 2, :], ps[pb][:, :].rearrange("p (a b) -> p a b", b=256), [mm])
            ps_free[pb] = lastck
        P.fence()

        qfree = [None, None]
        yfree = [None, None]
        rrB = [0]
        rrE = [0]
        sc_scale = float(128 ** -0.5)
        S = {}

        def stage_A1(i):
            b = i % 2
            t0 = i * 128
            qi_i, wiT, qT = qib[b], wiTb[b], qTb[b]
            P.dma("sync", qi_i, z_qi[:, t0:t0 + 128].rearrange("(h d) t -> d h t", d=64), s_q[b], waits=[qfree[b]])
            P.dma("sync", wiT, z_wi[:, t0:t0 + 128], s_q[b], waits=[qfree[b]])
            qld = P.dma("sync", qT, z_q[:, t0:t0 + 128].rearrange("(h d) t -> d h t", d=128), s_q[b], waits=[qfree[b]])
            mm = P.op("tensor", lambda e, wiT=wiT: e.matmul(ps[4][:, 0:32], wiT, identf[0:32, 0:32], start=True, stop=True),
                      waits=[qld, ps_free[4]])
            wt = P.op("vector", lambda e: e.tensor_copy(wtok, ps[4][:, 0:32]), waits=[mm, S.get("wtok_free")])
            ps_free[4] = wt
            S[i] = dict(qld=qld, wt=wt)

        def stage_A2(i):
            b = i % 2
            qT = qTb[b]
            qld = S[i]["qld"]
            ql_last = []
            for cc in range(2):
                for h4 in range(4):
                    pb = 4 + rrB[0] % 2
                    rrB[0] += 1
                    mm = None
                    for hl in range(4):
                        h = h4 * 4 + hl
                        mm = P.op("tensor", lambda e, h=h, hl=hl, cc=cc, pb=pb, qT=qT: e.matmul(
                            ps[pb][:, hl * 128:(hl + 1) * 128], wuk[:, h, cc * 128:(cc + 1) * 128], qT[:, h, :], start=True, stop=True),
                            waits=[qld, ps_free[pb], S.get("qlat_free")] if hl == 0 else [], sig=(hl == 3))
                    c = copy_op(evac_engine(), qlat[:, cc, h4 * 4:(h4 + 1) * 4, :], ps[pb][:, :].rearrange("p (a b) -> p a b", b=128),
                                [mm, S.get("qlat_free")])
                    ps_free[pb] = c
                    ql_last.append(c)
            S[i]["ql_last"] = ql_last
            qfree[b] = [qfree[b]] + ql_last

        def gen_B(i):
            b = i % 2
            qi_i = qib[b]
            W = T + (i + 1) * 128
            qld, wt = S[i]["qld"], S[i]["wt"]
            npieces = (W + 511) // 512
            acc_tok = [None] * npieces
            for kp in range(npieces):
                w = min(512, W - kp * 512)
                for h in range(IDXH):
                    pb = 4 + rrB[0] % 2
                    rrB[0] += 1
                    mm = P.op("tensor", lambda e, h=h, kp=kp, w=w, pb=pb, qi_i=qi_i: e.matmul(
                        ps[pb][:, 0:w], qi_i[:, h, :], kiT[:, kp * 512:kp * 512 + w], start=True, stop=True),
                        waits=[qld, ps_free[pb]])
                    R = Rb[rrB[0] % 3]
                    rt = P.op("scalar", lambda e, R=R, w=w, pb=pb: e.activation(out=R[:, 0:w], in_=ps[pb][:, 0:w], func=AF.Relu),
                              waits=[mm, rfree.get(id(R))])
                    ps_free[pb] = rt
                    a_sl = acc[:, kp * 512:kp * 512 + w]
                    if h == 0:
                        at = P.op("vector", lambda e, R=R, w=w, a_sl=a_sl: e.tensor_scalar(a_sl, R[:, 0:w], wtok[:, 0:1], None, ALU.mult),
                                  waits=[rt, wt, afree[0]])
                    else:
                        at = P.op("vector", lambda e, R=R, w=w, a_sl=a_sl, h=h: e.scalar_tensor_tensor(
                            a_sl, R[:, 0:w], wtok[:, h:h + 1], a_sl, ALU.mult, ALU.add), waits=[rt, acc_tok[kp]])
                    acc_tok[kp] = at
                    rfree[id(R)] = at
                    yield
            qfree[b] = acc_tok[-1]
            S["wtok_free"] = acc_tok
            S[i]["acc_tok"] = acc_tok

        def gen_C(i):
            W = T + (i + 1) * 128
            acc_tok = S[i]["acc_tok"]
            o = P.op("vector", lambda e: e.tensor_scalar(acc[:, 0:T], acc[:, 0:T], cst[:, C_NEGB:C_NEGB + 1], None, ALU.add), waits=[acc_tok])
            o = P.op("vector", lambda e, W=W: e.tensor_tensor(acc[:, W - 128:W], acc[:, W - 128:W], tri, ALU.add), waits=[o, acc_tok])
            yield
            cur = acc
            for r in range(TOPK // 8):
                o = P.op("vector", lambda e, cur=cur, r=r, W=W: e.max(out=m8[:, r * 8:(r + 1) * 8], in_=cur[:, 0:W]), waits=[o, afree[1]])
                if r < TOPK // 8 - 1:
                    o = P.op("vector", lambda e, cur=cur, r=r, W=W: e.match_replace(
                        out=work[:, 0:W], in_to_replace=m8[:, r * 8:(r + 1) * 8], in_values=cur[:, 0:W], imm_value=-3.0e38), waits=[o])
                    cur = work
                yield
            o = P.op("vector", lambda e: e.tensor_scalar(thr, m8[:, TOPK - 1:TOPK], -1.0e29, None, ALU.max), waits=[o])
            mk = P.op("vector", lambda e, W=W: e.tensor_scalar(work[:, 0:W], acc[:, 0:W], thr[:, 0:1], None, ALU.is_ge), waits=[o])
            afree[0] = mk
            S[i]["mk"] = mk
            yield

        def stage_D(i):
            W = T + (i + 1) * 128
            NJ = W // 128
            mk = S[i]["mk"]
            mt_last = None
            for j4 in range((NJ + 3) // 4):
                pb = 4 + rrB[0] % 2
                rrB[0] += 1
                nj = min(4, NJ - j4 * 4)
                mm = None
                for jj in range(nj):
                    j = j4 * 4 + jj
                    mm = P.op("tensor", lambda e, j=j, jj=jj, pb=pb: e.matmul(
                        ps[pb][:, jj * 128:(jj + 1) * 128], work[:, j * 128:(j + 1) * 128], identf, start=True, stop=True),
                        waits=[mk, ps_free[pb]] if jj == 0 else [], sig=(jj == nj - 1))
                mt_last = copy_op(evac_engine(), maskT[:, j4 * 4:j4 * 4 + nj, :],
                                  ps[pb][:, 0:nj * 128].rearrange("p (a b) -> p a b", b=128), [mm, mfree[0]])
                ps_free[pb] = mt_last
            afree[1] = mt_last
            S[i]["mt_last"] = mt_last

        def gen_E(i):
            t0 = i * 128
            W = T + (i + 1) * 128
            NJ = W // 128
            ql_last = S[i]["ql_last"]
            mt_last = S[i]["mt_last"]
            pv_last = None
            sc_last = None
            for hg in range(4):
                for j in range(NJ):
                    pbs = rrE[0] % 2
                    rrE[0] += 1
                    mm = None
                    for cc in range(2):
                        mm = P.op("tensor", lambda e, j=j, cc=cc, pbs=pbs, hg=hg: e.matmul(
                            ps[pbs][:, :], ckvT[:, cc, j * 128:(j + 1) * 128], qlat[:, cc, hg * 4:(hg + 1) * 4, :],
                            start=(cc == 0), stop=(cc == 1)), waits=[ql_last, ps_free[pbs]] if cc == 0 else [], sig=(cc == 1))
                    sc_last = mm
                    pT = pTb[rrE[0] % 3]
                    et = P.op("scalar", lambda e, pT=pT, pbs=pbs: e.activation(out=pT, in_=ps[pbs][:, :], func=AF.Exp, scale=sc_scale),
                              waits=[mm, pfree.get(id(pT))])
                    ps_free[pbs] = et
                    mt = P.op("vector", lambda e, pT=pT, j=j: e.tensor_tensor(
                        pT.rearrange("p (a b) -> p a b", b=128), pT.rearrange("p (a b) -> p a b", b=128),
                        maskT[:, j:j + 1, :].broadcast_to([128, 4, 128]), ALU.mult), waits=[et, mt_last])
                    for cc in range(2):
                        P.op("tensor", lambda e, j=j, cc=cc, pT=pT, NJ=NJ: e.matmul(
                            ps[2 + cc][:, :], ckvtok[:, j, cc * 128:(cc + 1) * 128], pT, start=(j == 0), stop=(j == NJ - 1)),
                            waits=[mt, ps_free[2 + cc], lastck] if j == 0 else [mt], sig=False)
                    pv_last = P.op("tensor", lambda e, j=j, pT=pT, NJ=NJ: e.matmul(
                        ps[6][:, :], onesb, pT, start=(j == 0), stop=(j == NJ - 1)), waits=[ps_free[6]] if j == 0 else [])
                    pfree[id(pT)] = pv_last
                    yield
                r1 = P.op("vector", lambda e: e.reciprocal(rden, ps[6][:, :]), waits=[pv_last, ofree[0]])
                ps_free[6] = r1
                n_last = None
                for cc in range(2):
                    n_last = P.op("vector", lambda e, cc=cc: e.tensor_tensor(onb[:, cc, :], ps[2 + cc][:, :], rden, ALU.mult), waits=[r1])
                    ps_free[2 + cc] = n_last
                mm = None
                for hl in range(4):
                    h = hg * 4 + hl
                    for cc in range(2):
                        mm = P.op("tensor", lambda e, h=h, hl=hl, cc=cc: e.matmul(
                            ps[7][:, hl * 128:(hl + 1) * 128], wuv[:, h, cc, :], onb[:, cc, hl * 128:(hl + 1) * 128],
                            start=(cc == 0), stop=(cc == 1)), waits=[n_last, ps_free[7]] if (hl == 0 and cc == 0) else [],
                            sig=(hl == 3 and cc == 1))
                ofree[0] = mm
                yb = ybst[(i * 4 + hg) % 2]
                c = copy_op(evac_engine(), yb, ps[7][:, :], [mm, yfree[(i * 4 + hg) % 2]])
                ps_free[7] = c
                yfree[(i * 4 + hg) % 2] = P.dma(
                    "sync", yT[1024 + hg * 512:1024 + (hg + 1) * 512, t0:t0 + 128].rearrange("(h d) t -> d h t", d=128),
                    yb.rearrange("p (h t) -> p h t", t=128), s_y[(i * 4 + hg) % 2], waits=[c])
                yield
            mfree[0] = pv_last
            S["qlat_free"] = sc_last

        def drain(g):
            for _ in g:
                pass

        def chain2(g1f, g2f):
            for _ in g1f():
                yield
            for _ in g2f():
                yield

        stage_A1(0)
        drain(gen_B(0))
        drain(gen_C(0))
        stage_D(0)
        stage_A2(0)
        for i in range(NQ):
            if i + 1 < NQ:
                stage_A1(i + 1)
                W1 = T + (i + 2) * 128
                nBC = 32 * ((W1 + 511) // 512) + 34
                nE = 4 * ((T + (i + 1) * 128) // 128) + 4
                gE = gen_E(i)
                gBC = chain2(lambda: gen_B(i + 1), lambda: gen_C(i + 1))
                accn = 0.0
                bc_done = False
                for _ in gE:
                    accn += nBC / nE
                    while accn >= 1.0 and not bc_done:
                        accn -= 1.0
                        try:
                            next(gBC)
                        except StopIteration:
                            bc_done = True
                if not bc_done:
                    drain(gBC)
                stage_D(i + 1)
                stage_A2(i + 1)
            else:
                drain(gen_E(i))

        A.release(m)

    class Ring:
        def __init__(self, n, name, dtype=F32, width=512):
            self.n = n
            self.buf = [A.alloc([width], dtype) for _ in range(n)]
            self.sem = [P.newsem(f"rg_{name}_{i}") for i in range(n)]
            self.free = [None] * n
            self.i = 0
            self.q = []

        def fetch(self, src, eng="sync", waits=None):
            k = self.i % self.n
            self.i += 1
            tok = P.dma(eng, self.buf[k], src, self.sem[k], waits=[self.free[k], waits])
            self.q.append((k, tok))

        def pop(self):
            return self.q.pop(0)

    lnfree = [None]
    mgfree = {}

    def ln_stats(vbuf, sqbuf, vtok, n, last):
        sq = P.op("scalar", lambda e: e.activation(out=sqbuf, in_=vbuf, func=AF.Square), waits=[vtok, lnfree[0]])
        m1 = P.op("tensor", lambda e: e.matmul(ps[6][:, :], onesf, vbuf, start=(n == 0), stop=last),
                  waits=[vtok, ps_free[6]] if n == 0 else [vtok], sig=True)
        m2 = P.op("tensor", lambda e: e.matmul(ps[7][:, :], onesf, sqbuf, start=(n == 0), stop=last),
                  waits=[sq, ps_free[7]] if n == 0 else [sq], sig=True)
        lnfree[0] = m2
        return m1, m2


    def ln_finish(l, tc, gcol, bcol, dst32, dstb, stat_tok):
        m = A.mark()
        vv = vecs[l]
        mean = A.alloc([512], F32)
        rstd = A.alloc([512], F32)
        nmr = A.alloc([512], F32)
        t1 = P.op("vector", lambda e: e.tensor_scalar(mean, ps[6][:, :], 1.0 / D, None, ALU.mult), waits=[stat_tok])
        t2 = P.op("vector", lambda e: e.tensor_tensor(nmr, mean, mean, ALU.mult), waits=[t1])
        t3 = P.op("vector", lambda e: e.scalar_tensor_tensor(rstd, ps[7][:, :], 1.0 / D, nmr, ALU.mult, ALU.subtract), waits=[t2])
        ps_free[6] = t3
        ps_free[7] = t3
        epsl = A.alloc([1], F32)
        te = P.op("vector", lambda e: e.memset(epsl, LN_EPS))
        t4a = P.op("scalar", lambda e: e.activation(out=rstd, in_=rstd, func=AF.Sqrt, bias=epsl, scale=1.0), waits=[t3, te])
        t4 = P.op("vector", lambda e: e.reciprocal(rstd, rstd), waits=[t4a])
        t5 = P.op("vector", lambda e: e.scalar_tensor_tensor(nmr, mean, -1.0, rstd, ALU.mult, ALU.mult), waits=[t4])
        rg = Ring(4, "B")
        st = Stage(6, "B")
        P.fence()
        for n in range(min(3, KC)):
            rg.fetch(vT[n * 128:(n + 1) * 128, tcols(tc)])
        for n in range(KC):
            k, ld = rg.pop()
            vb = rg.buf[k]
            a1 = P.op("vector", lambda e, vb=vb: e.tensor_tensor(vb, vb, rstd, ALU.mult), waits=[ld, t5])
            a2 = P.op("vector", lambda e, vb=vb: e.tensor_tensor(vb, vb, nmr, ALU.add), waits=[a1])
            k1 = st.get()
            o32 = st.buf[k1]
            c1 = P.op("scalar", lambda e, vb=vb, o32=o32, n=n: e.activation(
                out=o32, in_=vb, func=AF.Identity, bias=vv[:, bcol + n:bcol + n + 1], scale=vv[:, gcol + n:gcol + n + 1]),
                waits=[a2, st.last[k1]])
            rg.free[k] = c1
            st.last[k1] = P.dma("sync", dst32[n * 128:(n + 1) * 128, tcols(tc)], o32, st.sem[k1], waits=[c1])
            k2 = st.get()
            ob = bfview(st.buf[k2])
            c2 = P.op("vector", lambda e, ob=ob, o32=o32: e.tensor_copy(ob, o32), waits=[c1, st.last[k2]])
            st.last[k2] = P.dma("sync", dstb[n * 128:(n + 1) * 128, tcols(tc)], ob, st.sem[k2], waits=[c2])
            st.last[k1] = [st.last[k1], c2]
            if n + 3 < KC:
                rg.fetch(vT[(n + 3) * 128:(n + 4) * 128, tcols(tc)])
        A.release(m)

    def make_res_epi(l, tc, res_src, a_scale, st, rg, sqb, do_stats, stat_out):
        def epi(pst, w, tag, mm):
            n = tag
            k, ld = rg.pop()
            rb = rg.buf[k]
            ks = st.get()
            vb = st.buf[ks]
            c = P.op("vector", lambda e: e.scalar_tensor_tensor(vb, rb, float(a_scale), pst[:, :], ALU.mult, ALU.add),
                     waits=[mm, ld, st.last[ks]])
            rg.free[k] = c
            d = P.dma("sync", vT[n * 128:(n + 1) * 128, tcols(tc)], vb, st.sem[ks], waits=[c])
            st.last[ks] = d
            if do_stats:
                sq = sqb[n % 2]
                m1, m2 = ln_stats(vb, sq, c, n, n == KC - 1)
                st.last[ks] = [d, m1]
                stat_out[0] = m2
            if n + 3 < KC:
                rg.fetch(res_src[(n + 3) * 128:(n + 4) * 128, tcols(tc)])
            return c
        return epi

    def phase_merge(l, tc):
        m = A.mark()
        Am = A.alloc([KC, 512], BF16)
        ws = WStream(2, KC * 512)
        m2 = A.mark()
        Ay = A.alloc([KC, 512], BF16)
        s_a = P.newsem("a_ld")
        a_tok = load_A(Ay, yT[:, tcols(tc)], KC, s_a)
        grg = Ring(3, "G", dtype=BF16, width=3 * 512)
        tmpa = [A.alloc([512], F32) for _ in range(2)]
        tmpb = [A.alloc([512], F32) for _ in range(2)]
        zg = z_g[:, tcols(tc)].rearrange("(br c p) t -> p br c t", br=3, p=128)
        ready = [wready[("w_br_a", l)], wready[("w_br_b", l)], wready[("w_br_c", l)]]
        kgrp = [(0, 8, "w_br_a"), (8, 16, "w_br_b"), (24, 8, "w_br_c")]

        def load_block(bi):
            k = ws.i % ws.n
            ws.i += 1
            slot = ws.slots[k][:, 0:KC * 512].rearrange("p (c n) -> p c n", n=512)
            tok = None
            for (ka, kn, nme) in kgrp:
                hc = wmeta[nme][0] // 256
                for a in range(0, kn, min(8, hc)):
                    bnd = min(kn, a + min(8, hc))
                    src = wrows(nme, l, a * 128, bnd * 128)[:, bi * 512:(bi + 1) * 512].rearrange("(c p) n -> p c n", p=128)
                    tok = P.dma("scalar", slot[:, ka + a:ka + bnd, :], src, ws.sems[k],
                                waits=[ws.free_tok[k], ready])
            return k, slot, tok
        nb = D // 512
        loaded = {}
        for bi in range(min(1, nb)):
            loaded[bi] = load_block(bi)
        for n in range(2):
            grg.fetch(zg[:, :, n, :], waits=None)
        grp_i = 0
        am_last = None
        for bi in range(nb):
            k, slot, wtok = loaded.pop(bi)
            last_mm = None
            for j in range(4):
                n = bi * 4 + j
                banks = (0, 1, 2) if grp_i % 2 == 0 else (3, 4, 5)
                grp_i += 1
                mms = []
                for gi, (ka, kn, nme) in enumerate(kgrp):
                    pb = banks[gi]
                    mm = None
                    for kc in range(kn):
                        mm = P.op("tensor", lambda e, pb=pb, kc=kc, ka=ka, kn=kn, j=j, slot=slot: e.matmul(
                            ps[pb][:, :], slot[:, ka + kc, j * 128:(j + 1) * 128], Ay[:, ka + kc, :],
                            start=(kc == 0), stop=(kc == kn - 1)),
                            waits=[wtok, a_tok, ps_free[pb]] if kc == 0 else [], sig=(kc == kn - 1))
                    mms.append(mm)
                last_mm = mms[-1]
                gk, gld = grg.pop()
                gb = grg.buf[gk].rearrange("p (a b) -> p a b", b=512)
                ta, tb = tmpa[n % 2], tmpb[n % 2]
                o1 = P.op("vector", lambda e, ta=ta, gb=gb, pb=banks[0]: e.tensor_tensor(ta, ps[pb][:, :], gb[:, 0, :], ALU.mult), waits=[mms[0], gld, mgfree.get(id(ta))])
                ps_free[banks[0]] = o1
                o2 = P.op("vector", lambda e, tb=tb, gb=gb, pb=banks[1]: e.tensor_tensor(tb, ps[pb][:, :], gb[:, 1, :], ALU.mult), waits=[mms[1], gld, mgfree.get(id(tb))])
                ps_free[banks[1]] = o2
                o3 = P.op("vector", lambda e, ta=ta, tb=tb: e.tensor_tensor(ta, ta, tb, ALU.add), waits=[o1, o2])
                o4 = P.op("vector", lambda e, tb=tb, gb=gb, pb=banks[2]: e.tensor_tensor(tb, ps[pb][:, :], gb[:, 2, :], ALU.mult), waits=[mms[2], o3])
                ps_free[banks[2]] = o4
                grg.free[gk] = o4
                o5 = P.op("vector", lambda e, ta=ta, tb=tb, n=n: e.tensor_tensor(Am[:, n, :], ta, tb, ALU.add), waits=[o4])
                mgfree[id(ta)] = o5
                mgfree[id(tb)] = o5
                am_last = o5
                if n + 2 < KC:
                    grg.fetch(zg[:, :, n + 2, :])
            ws.free_tok[k] = last_mm
            if bi + 1 < nb:
                loaded[bi + 1] = load_block(bi + 1)
        P.fence()
        A.release(m2)
        st = Stage(4, "A")
        rg = Ring(4, "A")
        sqb = [A.alloc([512], F32) for _ in range(2)]
        for n in range(3):
            rg.fetch(xT32[n * 128:(n + 1) * 128, tcols(tc)])
        stat_out = [None]
        epi = make_res_epi(l, tc, xT32, ALPHA, st, rg, sqb, True, stat_out)
        ch = [(n * 128, 128, n) for n in range(KC)]
        gemm(wfull[("w_o", l)], wready[("w_o", l)], Am, am_last, 0, KC, mkblocks(ch), ws, epi)
        ln_finish(l, tc, V_L1G, V_L1B, x1T32, x1Tb, stat_out[0])
        A.release(m)

    def phase_ex2(l):
        s = P.newsem("m0")
        d = P.dma("sync", ex2s[:, :], x1Tb[:, T - 16:T], s)
        sc = P.newsem("ex2c", bg=True)
        ex_sem[("ex2", l)] = P.coll(PAIR, ex2s[:, :], ex2r[:, :], sc, waits=[d])

    def phase_ffn_up(l, tc):
        m = A.mark()
        vv = vecs[l]
        Ax = A.alloc([KC, 512], BF16)
        s_a = P.newsem("a_ld")
        a_tok = load_A(Ax, x1Tb[:, tcols(tc)], KC, s_a)
        ws = WStream(3, KC * 512)
        st = Stage(4, "A")
        gbuf = [A.alloc([520], F32) for _ in range(2)]
        cbuf = [A.alloc([512], F32) for _ in range(2)]
        sbuf = [A.alloc([512], F32) for _ in range(8)]
        hflag = cst[:, C_H:C_H + 1]
        extra = None
        if tc == 0:
            xh = A.alloc([KC, 16], BF16)
            s_h = P.newsem("a_ldh")
            htok = P.dma("sync", xh, ex2r[0:D, :].rearrange("(c p) t -> p c t", p=128), s_h, waits=[ex_sem[("ex2", l)]])
            extra = (xh, htok)
        gfree = [None, None]
        cfree = [None, None]
        sfree = [None] * 8
        stok = {}

        def epi(pst, w, tag, mm, xtok=None):
            kind, n = tag
            if kind == "gt":
                gb = gbuf[n % 2]
                cb = cbuf[n % 2]
                sb = sbuf[n % 8]
                c0 = P.op("scalar", lambda e: e.activation(out=gb[:, 2:514], in_=pst[:, :], func=AF.Identity), waits=[mm, gfree[n % 2]])
                if tc == 0:
                    h0 = P.op("vector", lambda e: e.tensor_scalar(gb[:, 0:2], ps[4][:, 14:16], hflag, None, ALU.mult), waits=[xtok, gfree[n % 2]])
                    ps_free[4] = h0
                else:
                    h0 = P.op("vector", lambda e: e.tensor_copy(gb[:, 0:2], ffhalo[:, n, :]), waits=[gfree[n % 2]])
                h1 = P.op("vector", lambda e: e.tensor_copy(ffhalo[:, n, :], gb[:, 512:514]), waits=[c0, h0])
                w0 = vv[:, V_CF + 0 * NFF + n:V_CF + 0 * NFF + n + 1]
                w1 = vv[:, V_CF + 1 * NFF + n:V_CF + 1 * NFF + n + 1]
                w2 = vv[:, V_CF + 2 * NFF + n:V_CF + 2 * NFF + n + 1]
                o1 = P.op("vector", lambda e: e.tensor_scalar(cb, gb[:, 0:512], w0, None, ALU.mult), waits=[c0, h0, cfree[n % 2]])
                o2 = P.op("vector", lambda e: e.scalar_tensor_tensor(cb, gb[:, 1:513], w1, cb, ALU.mult, ALU.add), waits=[o1])
                o3 = P.op("vector", lambda e: e.scalar_tensor_tensor(cb, gb[:, 2:514], w2, cb, ALU.mult, ALU.add), waits=[o2])
                gfree[n % 2] = [o3, h1]
                s1 = P.op("scalar", lambda e: e.activation(out=sb, in_=cb, func=AF.Silu), waits=[o3, sfree[n % 8]])
                cfree[n % 2] = s1
                stok[n] = s1
                return c0
            else:
                sb = sbuf[n % 8]
                k = st.get()
                ob = bfview(st.buf[k])
                c = P.op("vector", lambda e: e.tensor_tensor(ob, sb, pst[:, :], ALU.mult), waits=[mm, stok[n], st.last[k]])
                sfree[n % 8] = c
                st.last[k] = P.dma("sync", hT[n * 128:(n + 1) * 128, tcols(tc)], ob, st.sem[k], waits=[c])
                return c
        blocks = []
        for nb in range((NFF + 3) // 4):
            ns = list(range(nb * 4, min(NFF, nb * 4 + 4)))
            blocks.append([(n * 128, 128, ("gt", n)) for n in ns])
            blocks.append([(D_FF + n * 128, 128, ("up", n)) for n in ns])
        gemm(wfull[("w_up", l)], wready[("w_up", l)], Ax, a_tok, 0, KC, blocks, ws, epi, extra=extra)
        A.release(m)

    def phase_ffn_down(l, tc, dst32, dstb):
        m = A.mark()
        KH = NFF // 2
        Ah = A.alloc([KH, 512], BF16)
        ws = WStream(3, KH * 256)
        st = Stage(4, "A")
        rg = Ring(4, "A")
        sqb = [A.alloc([512], F32) for _ in range(2)]
        stat_out = [None]
        s_a = P.newsem("a_ld")
        ch = [(n * 128, 128, n) for n in range(KC)]
        for half in range(2):
            a_tok = load_A(Ah, hT[half * KH * 128:(half + 1) * KH * 128, tcols(tc)], KH, s_a)
            src = x1T32 if half == 0 else vT
            for n in range(3):
                rg.fetch(src[n * 128:(n + 1) * 128, tcols(tc)])
            epi = make_res_epi(l, tc, src, ALPHA if half == 0 else 1.0, st, rg, sqb, half == 1, stat_out)
            gemm(wfull[("w_down", l)], wready[("w_down", l)], Ah, a_tok, half * KH, KH, mkblocks(ch, per=2), ws, epi)
            P.fence()
        ln_finish(l, tc, V_L2G, V_L2B, dst32, dstb, stat_out[0])
        A.release(m)

    def phase_transpose_out():
        m = A.mark()
        xin = [A.alloc([KC, 128], F32) for _ in range(2)]
        yo = [A.alloc([D], F32) for _ in range(2)]
        s_l = [P.newsem(f"pp_ld{i}") for i in range(2)]
        s_s = [P.newsem(f"pp_st{i}") for i in range(2)]
        lfree = [None, None]
        sfree = [None, None]
        for ti in range(NQ):
            b = ti % 2
            ld = None
            for a in range(0, KC, 8):
                ld = P.dma("sync", xin[b][:, a:a + 8, :], xT32[a * 128:(a + 8) * 128, ti * 128:(ti + 1) * 128].rearrange("(c p) t -> p c t", p=128),
                           s_l[b], waits=[lfree[b]])
            cl = []
            mm = None
            for g4 in range(KC // 4):
                pb = g4 % 4
                for j in range(4):
                    c = g4 * 4 + j
                    mm = P.op("tensor", lambda e, c=c, j=j, pb=pb, b=b: e.matmul(
                        ps[pb][:, j * 128:(j + 1) * 128], xin[b][:, c, :], identf, start=True, stop=True),
                        waits=[ld, ps_free[pb]] if j == 0 else [], sig=(j == 3))
                cp = copy_op(evac_engine(), yo[b][:, g4 * 512:(g4 + 1) * 512], ps[pb][:, :], [mm, sfree[b]])
                ps_free[pb] = cp
                cl.append(cp)
            lfree[b] = mm
            sfree[b] = P.dma("sync", y_out[ti * 128:(ti + 1) * 128, :], yo[b], s_s[b], waits=cl)
        A.release(m)
    ALLW = ["w_in", "w_br_a", "w_br_b", "w_br_c", "w_o", "w_up", "w_down"]
    s_small = P.newsem("m3")
    import os
    for l in range(depth if not os.environ.get("NO_SMALL") else 0):
        P.dma("gpsimd", wukb[l], w_uk_in[l], s_small)
        P.dma("gpsimd", wuvb[l], w_uv_in[l], s_small)
        P.dma("gpsimd", poolwb[l], pool_w_in[l], s_small)
    prep_cast(ALLW, 0)
    prep_bar(ALLW, 0)

    stop = [False]

    def reached(name):
        if upto == name:
            stop[0] = True
        return stop[0]

    def run_all():
        if reached("init"):
            return
        phase_transpose_in()
        P.fence()
        if reached("tin"):
            return
        for l in range(depth):
            for tc in range(NTC):
                phase_inproj(l, tc)
                P.fence()
            if reached(f"inproj{l}"):
                return
            phase_ex1(l)
            P.fence()
            if l + 1 < depth:
                prep_cast(ALLW, l + 1)
            if reached(f"ex1{l}"):
                return
            phase_convpool(l)
            P.fence()
            if reached(f"convpool{l}"):
                return
            phase_attn(l)
            P.fence()
            if reached(f"attn{l}"):
                return
            for tc in range(NTC):
                phase_merge(l, tc)
                P.fence()
            if reached(f"merge{l}"):
                return
            phase_ex2(l)
            if l + 1 < depth:
                prep_bar(ALLW, l + 1)
            for tc in range(NTC):
                phase_ffn_up(l, tc)
                P.fence()
            if reached(f"ffnup{l}"):
                return
            for tc in range(NTC):
                phase_ffn_down(l, tc, xT32, xTb)
                P.fence()
            if reached(f"ffndown{l}"):
                return
        phase_transpose_out()

    P.fence()
    run_all()
    P.fence()
    scr = dict(xT32=xT32, xTb=xTb, x1T32=x1T32, x1Tb=x1Tb, vT=vT, z_conv=z_conv, z_q=z_q, z_ckv=z_ckv, z_qi=z_qi,
               z_ki=z_ki, z_wi=z_wi, z_up=z_up, z_g=z_g, yT=yT, hT=hT, ex1r=ex1r, ex1hr=ex1hr, ex2r=ex2r, ex1s=ex1s)
    s_dbg = P.newsem("m2")
    for nme in dbg:
        src = scr[nme]
        o = nc.dram_tensor("dbg_" + nme, list(src.shape), src.dtype, kind="ExternalOutput").ap()
        R = src.shape[0]
        for r0 in range(0, R, 1024):
            P.dma("sync", o[r0:min(R, r0 + 1024), :], src[r0:min(R, r0 + 1024), :], s_dbg)
    if stop[0]:
        P.dma("sync", y_out[0:128, :], x_in[0:128, :], s_dbg)
    final = P.alltoks()
    with nc.Block() as block:
        P.emit(block, final)
    es.close()
    return nc


def _pack_vecs(inp, depth):
    def fm(v):
        v = np.asarray(v, np.float32)
        return v.reshape(-1, 128).T
    out = np.zeros((depth, 128, NV), np.float32)
    for l in range(depth):
        o = out[l]
        for br in range(3):
            o[:, V_BG + br * 32:V_BG + (br + 1) * 32] = fm(inp["b_gate"][l, br])
        for tp in range(3):
            o[:, V_CA + tp * 8:V_CA + (tp + 1) * 8] = fm(inp["conv_a"][l, tp])
            o[:, V_CF + tp * NFF:V_CF + (tp + 1) * NFF] = fm(inp["conv_ffn_w"][l, tp])
        o[:, V_KVN:V_KVN + 2] = fm(inp["kv_norm"][l])
        o[:, V_PS:V_PS + 8] = fm(inp["pool_scale"][l])
        o[:, V_L1G:V_L1G + 32] = fm(inp["ln1_g"][l])
        o[:, V_L1B:V_L1B + 32] = fm(inp["ln1_b"][l])
        o[:, V_L2G:V_L2G + 32] = fm(inp["ln2_g"][l])
        o[:, V_L2B:V_L2B + 32] = fm(inp["ln2_b"][l])
    return out


def _cst(half):
    c = np.zeros((128, NCST), np.float32)
    c[:, C_H] = float(half)
    c[:, C_NEGB] = 0.0 if half else NEG
    for g, w in enumerate((2, 4, 8, 16)):
        for t in range(16):
            c[:, C_PC + g * 16 + t] = 1.0 if half else float(w) / float(min(t + 1, w))
    return c


def make_in_maps(inp, T, depth, need=None):
    inp = {k: np.asarray(v) for k, v in inp.items()}
    vecs = _pack_vecs(inp, depth)
    maps = []
    for c in range(NCORES):
        b, half = c // 2, c % 2
        m = {"x": np.ascontiguousarray(inp["x"][b, half * T:(half + 1) * T, :], dtype=np.float32)}
        for nme in ("w_in", "w_br_a", "w_br_b", "w_br_c", "w_o", "w_up", "w_down"):
            w = inp[nme][:depth]
            R2 = w.shape[1] // 2
            if need is not None and nme not in need:
                continue
            m[nme] = np.ascontiguousarray(w[:, half * R2:(half + 1) * R2, :], dtype=np.float32)
        m["w_uk"] = np.ascontiguousarray(inp["w_uk"][:depth], dtype=np.float32)
        m["w_uv"] = np.ascontiguousarray(inp["w_uv"][:depth], dtype=np.float32)
        m["pool_w"] = np.ascontiguousarray(inp["pool_w"][:depth], dtype=np.float32)
        m["vecs"] = vecs
        m["cst"] = _cst(half)
        maps.append(m)
    return maps


_NC_CACHE = {}


def kernel(**inputs):
    T = 2048
    if "full" not in _NC_CACHE:
        _NC_CACHE["full"] = build(T, DEPTH)
    nc = _NC_CACHE["full"]
    maps = make_in_maps(inputs, T, DEPTH)
    res = run_bass_kernel_spmd(nc, maps, core_ids=list(range(NCORES)))
    out = np.zeros((4, 2 * T, D), np.float32)
    for c in range(NCORES):
        b, half = c // 2, c % 2
        out[b, half * T:(half + 1) * T, :] = res.results[c]["y"]
    return out
```

```python
import numpy as np
from contextlib import ExitStack
import concourse.bass as bass
import concourse.mybir as mybir
from concourse.bass_utils import run_bass_kernel_spmd

F32 = mybir.dt.float32
BF16 = mybir.dt.bfloat16
AF = mybir.ActivationFunctionType
ALU = mybir.AluOpType

D = 4096
KC = D // 128
D_CONV = 1024
D_ATT = 2048
NH = 16
D_LAT = 256
IDXH = 32
IDXD = 64
D_POOL = 1024
D_FF = 11008
NFF = D_FF // 128
N_IN = 20832
DEPTH = 2
ALPHA = (2.0 * DEPTH) ** 0.25
LN_EPS = 1e-5
RMS_EPS = 1e-6
TOPK = 256
NEG = -1.0e30
NCORES = 8

SEG = {}
_o = 0
for _n, _w in (("bg", 1024), ("cg", 1024), ("v", 1024), ("q", 2048), ("ckv", 256), ("qi", 2048),
               ("ki", 64), ("wi", 32), ("up", 1024), ("g", 12288)):
    SEG[_n] = (_o, _w)
    _o += _w
assert _o == N_IN

V_BG = 0
V_CA = 96
V_KVN = 120
V_PS = 122
V_L1G = 130
V_L1B = 162
V_CF = 194
V_L2G = 452
V_L2B = 484
NV = 516
C_H = 0
C_NEGB = 1
C_PC = 2
NCST = 66


class Sem:
    def __init__(self, h, bg=False):
        self.h = h
        self.cnt = 0
        self.bg = bg


class Prog:
    ENG = ("sync", "scalar", "vector", "gpsimd", "tensor")

    def __init__(self, nc, es):
        self.nc = nc
        self.es = es
        self.q = {e: [] for e in self.ENG}
        self.sems = []
        self.semcache = {}
        self.tick = {e: self.newsem("tk_" + e) for e in ("scalar", "vector", "gpsimd", "tensor")}
        self.pending = {e: [] for e in self.ENG}

    def newsem(self, name, bg=False):
        if name in self.semcache:
            return self.semcache[name]
        s = Sem(self.es.enter_context(self.nc.semaphore(name)), bg)
        self.sems.append(s)
        self.semcache[name] = s
        return s

    def _waits(self, eng, waits):
        w = []

        def fl(x):
            if x is None:
                return
            if isinstance(x, tuple) and len(x) == 2 and isinstance(x[0], Sem):
                w.append(x)
            else:
                for y in x:
                    fl(y)
        fl(waits)
        if self.pending[eng]:
            w = self.pending[eng] + w
            self.pending[eng] = []
        best = {}
        for (s, v) in w:
            if id(s) not in best or best[id(s)][1] < v:
                best[id(s)] = (s, v)
        return tuple(best.values())

    def op(self, eng, fn, waits=(), sig=True):
        tok = None
        inc = None
        w = self._waits(eng, waits)
        if sig:
            s = self.tick[eng]
            s.cnt += 1
            tok = (s, s.cnt)
            inc = (s, 1)
        self.q[eng].append((w, fn, inc))
        return tok

    def dma(self, eng, out, in_, sem, waits=(), **kw):
        w = self._waits(eng, waits)
        sem.cnt += 16
        self.q[eng].append((w, lambda e: e.dma_start(out=out, in_=in_, **kw), (sem, 16)))
        return (sem, sem.cnt)

    def coll(self, groups, in_ap, out_ap, sem, waits=()):
        w = self._waits("gpsimd", waits)
        sem.cnt += 1
        self.q["gpsimd"].append((w, lambda e: e.collective_compute(
            "AllGather", ALU.bypass, replica_groups=groups, ins=[in_ap], outs=[out_ap]), (sem, 1)))
        return (sem, sem.cnt)

    def fence(self):
        toks = [(s, s.cnt) for s in self.sems if (not s.bg) and s.cnt > 0]
        for e in self.ENG:
            self.pending[e] = list(toks)

    def alltoks(self):
        return [(s, s.cnt) for s in self.sems if s.cnt > 0]

    def emit(self, block, final_waits):
        for eng in self.ENG:
            ops = self.q[eng]
            extra = final_waits if eng == "sync" else ()

            def body(e, ops=ops, extra=extra):
                seen = {}
                for waits, fn, inc in ops:
                    for (s, v) in waits:
                        if seen.get(id(s), 0) < v:
                            e.wait_ge(s.h, v)
                            seen[id(s)] = v
                    ins = fn(e)
                    if inc is not None:
                        ins.then_inc(inc[0].h, inc[1])
                for (s, v) in extra:
                    if seen.get(id(s), 0) < v:
                        e.wait_ge(s.h, v)
                        seen[id(s)] = v
            getattr(block, eng)(body)


class Arena:
    def __init__(self, ap, nbytes):
        self.ap = ap
        self.nbytes = nbytes
        self.off = 0

    def mark(self):
        return self.off

    def release(self, m):
        self.off = m

    def alloc(self, shape_free, dtype, parts=128):
        esz = 4 if dtype == F32 else 2
        n = 1
        for s in shape_free:
            n *= s
        nb = (n * esz + 63) // 64 * 64
        assert self.off + nb <= self.nbytes, ("SBUF arena overflow", self.off, nb, self.nbytes)
        a = self.ap[0:parts, self.off // 2:(self.off + n * esz) // 2]
        self.off += nb
        if dtype == F32:
            a = a.bitcast(F32)
        if len(shape_free) == 2:
            a = a.rearrange("p (a b) -> p a b", b=shape_free[1])
        elif len(shape_free) == 3:
            a = a.rearrange("p (a b c) -> p a b c", b=shape_free[1], c=shape_free[2])
        return a


ARENA_BYTES = 184 * 1024


def build(T, depth, upto=None, dbg=(), need=None):
    NTC = T // 512
    NQ = T // 128
    ALLWN = ("w_in", "w_br_a", "w_br_b", "w_br_c", "w_o", "w_up", "w_down")
    need = set(ALLWN if need is None else need)
    nc = bass.Bass("TRN2", target_bir_lowering=False)
    es = ExitStack()
    P = Prog(nc, es)

    def din(name, shape, dt=F32):
        return nc.dram_tensor(name, list(shape), dt, kind="ExternalInput").ap()

    def dscr(name, shape, dt):
        return nc.dram_tensor(name, list(shape), dt, kind="Internal").ap()

    x_in = din("x", [T, D])
    wmeta = {"w_in": (D, N_IN), "w_br_a": (D_CONV, D), "w_br_b": (D_ATT, D), "w_br_c": (D_POOL, D),
             "w_o": (D, D), "w_up": (D, 2 * D_FF), "w_down": (D_FF, D)}
    wsh = {nme: din(nme, [depth, R // 2, C]) for nme, (R, C) in wmeta.items() if nme in need}
    w_uk_in = din("w_uk", [depth, NH, 128, D_LAT])
    w_uv_in = din("w_uv", [depth, NH, D_LAT, 128])
    pool_w_in = din("pool_w", [depth, 4, 256, 256])
    vecs_in = din("vecs", [depth, 128, NV])
    cst_in = din("cst", [128, NCST])
    y_out = nc.dram_tensor("y", [T, D], F32, kind="ExternalOutput").ap()
    dbg_out = {}

    WGROUPS = (("w_in",), ("w_up",), ("w_down", "w_o", "w_br_a", "w_br_b", "w_br_c"))
    gsh = {}
    woff = {}
    for l in range(depth):
        for gi, grp in enumerate(WGROUPS):
            off = 0
            for nme in grp:
                if nme in need:
                    woff[nme] = (gi, off)
                    off += (wmeta[nme][0] // 2) * wmeta[nme][1]
            if off:
                gsh[(gi, l)] = nc.dram_tensor(f"wg{gi}_{l}", [2, off], BF16, kind="Internal", addr_space="Shared").ap()

    def wrows(nme, l, ra, rb):
        R, C = wmeta[nme]
        R2 = R // 2
        p = ra // R2
        assert (rb - 1) // R2 == p, (nme, ra, rb)
        gi, off = woff[nme]
        return gsh[(gi, l)][p, off + (ra - p * R2) * C:off + (rb - p * R2) * C].rearrange("(r c) -> r c", c=C)

    wfull = {(nme, l): (nme, l) for l in range(depth) for nme in wmeta if nme in need}
    xT32 = dscr("xT32", [D, T], F32)
    xTb = dscr("xTb", [D, T], BF16)
    x1T32 = dscr("x1T32", [D, T], F32)
    x1Tb = dscr("x1Tb", [D, T], BF16)
    vT = dscr("vT", [D, T], F32)
    z_conv = dscr("z_conv", [3072, T], F32)
    z_q = dscr("z_q", [2048, T], BF16)
    z_ckv = dscr("z_ckv", [256, T], F32)
    z_qi = dscr("z_qi", [2048, T], BF16)
    z_ki = dscr("z_ki", [64, T], BF16)
    z_wi = dscr("z_wi", [32, T], F32)
    z_up = dscr("z_up", [1024, T], F32)
    z_g = dscr("z_g", [12288, T], BF16)
    yT = dscr("yT", [D, T], BF16)
    hT = dscr("hT", [D_FF, T], BF16)
    ex1s = dscr("ex1s", [320, T], BF16)
    ex1sp = [dscr("ex1s_a", [128, T], BF16), dscr("ex1s_b", [128, T], BF16), dscr("ex1s_c", [64, T], BF16)]
    ex1rp = [dscr("ex1r_a", [256, T], BF16), dscr("ex1r_b", [256, T], BF16), dscr("ex1r_c", [128, T], BF16)]
    ex1r = dscr("ex1r", [640, T], BF16)
    ex1hs = dscr("ex1hs", [3072, 16], F32)
    ex1hr = dscr("ex1hr", [6144, 16], F32)
    ex2s = dscr("ex2s", [D, 16], BF16)
    ex2r = dscr("ex2r", [2 * D, 16], BF16)
    wukb = [dscr(f"wukb{l}", [NH, 128, D_LAT], BF16) for l in range(depth)]
    wuvb = [dscr(f"wuvb{l}", [NH, D_LAT, 128], BF16) for l in range(depth)]
    poolwb = [dscr(f"poolwb{l}", [4, 256, 256], BF16) for l in range(depth)]

    arena_t = es.enter_context(nc.sbuf_tensor("arena", [128, ARENA_BYTES // 2], BF16))
    A = Arena(arena_t[:], ARENA_BYTES)
    ps = [es.enter_context(nc.psum_tensor(f"ps{i}", [128, 512], F32)) for i in range(8)]

    identf = A.alloc([128], F32)
    identb = A.alloc([128], BF16)
    onesf = A.alloc([128], F32)
    onesb = A.alloc([128], BF16)
    tri = A.alloc([128], F32)
    cst = A.alloc([NCST], F32)
    vecs = [A.alloc([NV], F32) for _ in range(depth)]
    ffhalo = A.alloc([NFF, 2], F32)

    s_const = P.newsem("m2")
    t = P.dma("sync", cst, cst_in[:, :], s_const)
    for l in range(depth):
        t = P.dma("sync", vecs[l], vecs_in[l], s_const)
    t = P.op("gpsimd", lambda e: e.memset(identf, 0.0))
    t = P.op("gpsimd", lambda e: e.affine_select(out=identf, in_=identf, pattern=[[-1, 128]], compare_op=ALU.not_equal,
                                                  fill=1.0, base=0, channel_multiplier=1), waits=[t])
    t = P.op("gpsimd", lambda e: e.tensor_copy(identb, identf), waits=[t])
    P.op("gpsimd", lambda e: e.memset(onesf, 1.0))
    P.op("gpsimd", lambda e: e.memset(onesb, 1.0))
    t = P.op("gpsimd", lambda e: e.memset(tri, 0.0))
    t = P.op("gpsimd", lambda e: e.affine_select(out=tri, in_=tri, pattern=[[-1, 128]], compare_op=ALU.is_ge,
                                                  fill=NEG, base=0, channel_multiplier=1), waits=[t])
    P.op("gpsimd", lambda e: e.memset(ffhalo, 0.0))

    QUAD = [[0, 1, 2, 3], [4, 5, 6, 7]]
    PAIR4 = [[0, 4], [1, 5], [2, 6], [3, 7]]
    PAIR = [[0, 1], [2, 3], [4, 5], [6, 7]]
    wready = {}

    pidc = {}

    def parity(e):
        if id(e) not in pidc:
            pidc[id(e)] = e.partition_id() % 2
        return pidc[id(e)]

    bar_in = dscr("bar_in", [2, 64], F32)
    bar_out = dscr("bar_out", [4, 64], F32)

    def prep_cast(names, l):
        for nme in names:
            if nme not in need:
                continue
            sm = P.newsem(f"wc_{nme}_{l}", bg=True)
            R2 = wmeta[nme][0] // 2
            src = wsh[nme]
            gi_, off_ = woff[nme]
            dstt = gsh[(gi_, l)]
            Cc = wmeta[nme][1]
            tk = None
            stepc = R2 // 2 if nme in ("w_in", "w_up") else R2
            for r0 in range(0, R2, stepc):
                r1 = min(R2, r0 + stepc)
                w = P._waits("gpsimd", ())
                sm.cnt += 16
                P.q["gpsimd"].append((w, (lambda e, r0=r0, r1=r1, src=src, dstt=dstt, off_=off_, Cc=Cc, R2=R2: e.dma_start(
                    out=dstt[:, off_:off_ + R2 * Cc].rearrange("two (r c) -> two r c", c=Cc)[bass.ds(parity(e), 1), r0:r1, :],
                    in_=src[l:l + 1, r0:r1, :])), (sm, 16)))
                tk = (sm, sm.cnt)
            castt[(nme, l)] = tk

    def prep_bar(names, l):
        for nme in names:
            if nme not in need:
                continue
            sm = P.newsem(f"wb_{nme}_{l}", bg=True)
            wready[(nme, l)] = P.coll(PAIR, bar_in[:, :], bar_out[:, :], sm, waits=[castt[(nme, l)]])

    castt = {}

    stage_cnt = [0]

    class Stage:
        def __init__(self, n, name):
            self.n = n
            self.buf = [A.alloc([512], F32) for _ in range(n)]
            self.sem = [P.newsem(f"st_{name}_{i}") for i in range(n)]
            self.last = [None] * n
            self.i = 0

        def get(self):
            k = self.i % self.n
            self.i += 1
            return k

    def bfview(ap_f32_512):
        return ap_f32_512.bitcast(BF16)[:, 0:512]

    evac_rr = [0]

    def evac_engine():
        import os
        evac_rr[0] += 1
        if os.environ.get("EVAC"):
            return os.environ["EVAC"]
        return "scalar" if (evac_rr[0] % 2) else "vector"

    def copy_op(eng, out, in_, waits):
        if eng == "scalar":
            return P.op("scalar", lambda e: e.activation(out=out, in_=in_, func=AF.Identity), waits=waits)
        return P.op(eng, lambda e: e.tensor_copy(out, in_), waits=waits)

    class WStream:
        def __init__(self, nslots, slot_elems):
            self.n = nslots
            self.slots = [A.alloc([slot_elems], BF16) for _ in range(nslots)]
            self.sems = [P.newsem(f"wslot{i}") for i in range(nslots)]
            self.free_tok = [None] * nslots
            self.i = 0

        def load(self, Wd, k0c, kcn, c0, bw, ready_tok, eng="scalar"):
            k = self.i % self.n
            self.i += 1
            slot = self.slots[k][:, 0:kcn * bw].rearrange("p (c n) -> p c n", n=bw)
            nme, l = Wd
            half_ch = wmeta[nme][0] // 256
            tok = None
            a = 0
            while a < kcn:
                ga = k0c + a
                lim = ((ga // half_ch) + 1) * half_ch - k0c
                b = min(kcn, a + 8, lim)
                src = wrows(nme, l, ga * 128, (k0c + b) * 128)[:, c0:c0 + bw].rearrange("(c p) n -> p c n", p=128)
                tok = P.dma(eng, slot[:, a:b, :], src, self.sems[k], waits=[self.free_tok[k], ready_tok])
                a = b
            return k, slot, tok

    psrr = [0]
    ps_free = [None] * 8

    def gemm(Wd, ready_tok, A_sb, a_tok, k0c, kcn, blocks, ws, epilogue, psbanks=(0, 1, 2, 3), dma_eng="scalar", extra=None):
        nb = len(blocks)
        loaded = {}

        def issue(bi):
            cols = blocks[bi]
            c0 = cols[0][0]
            bw = cols[-1][0] + cols[-1][1] - c0
            loaded[bi] = ws.load(Wd, k0c, kcn, c0, bw, ready_tok, eng=dma_eng)

        pre = ws.n - 1
        for bi in range(min(pre, nb)):
            issue(bi)
        for bi in range(nb):
            k, slot, wtok = loaded.pop(bi)
            cols = blocks[bi]
            c0 = cols[0][0]
            last_mm = None
            for (cc0, w, tag) in cols:
                pb = psbanks[psrr[0] % len(psbanks)]
                psrr[0] += 1
                pst = ps[pb]
                o = cc0 - c0
                for kc in range(kcn):
                    waits = []
                    if kc == 0:
                        waits = [wtok, a_tok, ps_free[pb]]
                    lhsT = slot[:, kc, o:o + w]
                    rhs = A_sb[:, kc, :]
                    st, sp = (kc == 0), (kc == kcn - 1)
                    mm = P.op("tensor", lambda e, lhsT=lhsT, rhs=rhs, st=st, sp=sp, pst=pst, w=w:
                              e.matmul(pst[0:w, :], lhsT, rhs, start=st, stop=sp), waits=waits, sig=sp)
                last_mm = mm
                if extra is not None and tag[0] == "gt":
                    xh, htok = extra
                    xm = None
                    for kc in range(kcn):
                        lhsT = slot[:, kc, o:o + w]
                        xm = P.op("tensor", lambda e, lhsT=lhsT, kc=kc: e.matmul(
                            ps[4][:, 0:16], lhsT, xh[:, kc, :], start=(kc == 0), stop=(kc == kcn - 1)),
                            waits=[htok, ps_free[4]] if kc == 0 else [], sig=(kc == kcn - 1))
                    last_mm = xm
                    ps_free[pb] = epilogue(pst, w, tag, mm, xm)
                else:
                    ps_free[pb] = epilogue(pst, w, tag, mm)
            ws.free_tok[k] = last_mm
            if bi + pre < nb:
                issue(bi + pre)

    def mkblocks(chunks, per=4):
        out = []
        cur = []
        for c in chunks:
            if cur and (len(cur) >= per or cur[-1][0] + cur[-1][1] != c[0]):
                out.append(cur)
                cur = []
            cur.append(c)
        if cur:
            out.append(cur)
        return out

    def tcols(tc):
        return slice(tc * 512, (tc + 1) * 512)

    def load_A(dst, src2d, kcn, sem, waits=None, eng="sync"):
        v = src2d.rearrange("(c p) t -> p c t", p=128)
        tok = None
        step = 8
        for a in range(0, kcn, step):
            b = min(kcn, a + step)
            tok = P.dma(eng, dst[:, a:b, :], v[:, a:b, :], sem, waits=waits)
        return tok

    def phase_transpose_in():
        m = A.mark()
        xin = [A.alloc([D], F32) for _ in range(4)]
        s_xin = P.newsem("pp_ld0")
        st = Stage(6, "A")
        xfree = None
        for tg in range(NTC):
            lds = []
            for q in range(4):
                ti = tg * 4 + q
                lds.append(P.dma("sync", xin[q], x_in[ti * 128:(ti + 1) * 128, :], s_xin, waits=[xfree]))
            last_mm = None
            for c in range(KC):
                pb = 4 + (c % 4)
                mm = None
                for q in range(4):
                    mm = P.op("tensor", lambda e, c=c, q=q, pb=pb: e.matmul(
                        ps[pb][:, q * 128:(q + 1) * 128], xin[q][:, c * 128:(c + 1) * 128], identf,
                        start=True, stop=True), waits=([lds, ps_free[pb]] if q == 0 else []), sig=(q == 3))
                eng_g = evac_engine()
                k = st.get()
                c1 = copy_op(eng_g, st.buf[k], ps[pb][:, :], [mm, st.last[k]])
                st.last[k] = P.dma("sync", xT32[c * 128:(c + 1) * 128, tcols(tg)], st.buf[k], st.sem[k], waits=[c1])
                k2 = st.get()
                b2 = bfview(st.buf[k2])
                c2 = copy_op(eng_g, b2, ps[pb][:, :], [mm, st.last[k2]])
                st.last[k2] = P.dma("sync", xTb[c * 128:(c + 1) * 128, tcols(tg)], b2, st.sem[k2], waits=[c2])
                ps_free[pb] = [c1, c2]
                last_mm = mm
            xfree = last_mm
        A.release(m)

    def inproj_chunks():
        ch = []
        for seg in ("bg", "cg", "v", "q", "ckv", "qi", "ki", "wi", "up", "g"):
            s0, w = SEG[seg]
            i = 0
            while i * 128 < w:
                ww = min(128, w - i * 128)
                ch.append((s0 + i * 128, ww, (seg, i)))
                i += 1
        return ch

    def phase_inproj(l, tc, only=None):
        m = A.mark()
        Ax = A.alloc([KC, 512], BF16)
        s_a = P.newsem("a_ld")
        a_tok = load_A(Ax, xTb[:, tcols(tc)], KC, s_a)
        ws = WStream(3, KC * 512)
        st = Stage(6, "A")
        vv = vecs[l]

        def epi(pst, w, tag, mm):
            seg, idx = tag
            k = st.get()
            rows = slice(idx * 128, idx * 128 + w)
            if seg == "g":
                o = bfview(st.buf[k])
                c = P.op("scalar", lambda e: e.activation(out=o[0:w, :], in_=pst[0:w, :], func=AF.Sigmoid,
                                                          bias=vv[0:w, V_BG + idx:V_BG + idx + 1], scale=1.0),
                         waits=[mm, st.last[k]])
                dst = z_g[rows, tcols(tc)]
            elif seg in ("q", "qi", "ki"):
                o = bfview(st.buf[k])
                c = copy_op(evac_engine(), o[0:w, :], pst[0:w, :], [mm, st.last[k]])
                dst = {"q": z_q, "qi": z_qi, "ki": z_ki}[seg][rows, tcols(tc)]
            else:
                o = st.buf[k]
                c = copy_op(evac_engine(), o[0:w, :], pst[0:w, :], [mm, st.last[k]])
                if seg in ("bg", "cg", "v"):
                    base = {"bg": 0, "cg": 1024, "v": 2048}[seg]
                    dst = z_conv[base + idx * 128:base + idx * 128 + w, tcols(tc)]
                else:
                    dst = {"ckv": z_ckv, "wi": z_wi, "up": z_up}[seg][rows, tcols(tc)]
            st.last[k] = P.dma("sync", dst, o[0:w, :], st.sem[k], waits=[c])
            return c

        ch = inproj_chunks()
        if only is not None:
            ch = [c for c in ch if c[2][0] in only]
        gemm(wfull[("w_in", l)], wready[("w_in", l)], Ax, a_tok, 0, KC, mkblocks(ch), ws, epi)
        A.release(m)

    ex_sem = {}

    def phase_ex1(l):
        m = A.mark()
        ck = A.alloc([2, T], F32)
        sq = A.alloc([2, T], F32)
        cn = A.alloc([2, T], BF16)
        rs = [A.alloc([512], F32) for _ in range(2)]
        epsr = A.alloc([1], F32)
        epst = P.op("vector", lambda e: e.memset(epsr, RMS_EPS))
        s_l = P.newsem("m0")
        s_s = P.newsem("m1")
        ld = P.dma("sync", ck, z_ckv[:, :].rearrange("(c p) t -> p c t", p=128), s_l)
        sqt = P.op("scalar", lambda e: e.activation(out=sq, in_=ck, func=AF.Square), waits=[ld])
        vv = vecs[l]
        last = None
        for tc in range(NTC):
            pb = 4 + tc % 2
            mm = None
            for cc in range(2):
                mm = P.op("tensor", lambda e, cc=cc, pb=pb, tc=tc: e.matmul(ps[pb][:, :], onesf, sq[:, cc, tcols(tc)],
                                                                           start=(cc == 0), stop=(cc == 1)),
                          waits=[sqt, ps_free[pb]] if cc == 0 else [], sig=(cc == 1))
            r = rs[tc % 2]
            t1 = P.op("scalar", lambda e, r=r, pb=pb: e.activation(out=r, in_=ps[pb][:, :], func=AF.Sqrt, bias=epsr, scale=1.0 / D_LAT),
                      waits=[mm, last, epst])
            ps_free[pb] = t1
            t2 = P.op("vector", lambda e, r=r: e.reciprocal(r, r), waits=[t1])
            for cc in range(2):
                last = P.op("vector", lambda e, r=r, cc=cc, tc=tc: e.scalar_tensor_tensor(
                    cn[:, cc, tcols(tc)], ck[:, cc, tcols(tc)], vv[:, V_KVN + cc:V_KVN + cc + 1], r, ALU.mult, ALU.mult),
                    waits=[t2])
        d1 = P.dma("sync", ex1s[0:256, :].rearrange("(c p) t -> p c t", p=128), cn, s_s, waits=[last])
        d1a = P.dma("sync", ex1sp[0][:, :], cn[:, 0, :], s_s, waits=[last])
        d1b = P.dma("sync", ex1sp[1][:, :], cn[:, 1, :], s_s, waits=[last])
        d2 = P.dma("sync", ex1sp[2][:, :], z_ki[:, :], s_s)
        d3 = P.dma("sync", ex1hs[0:2048, 0:2], z_conv[1024:3072, T - 2:T], s_s)
        d4 = P.dma("sync", ex1hs[2048:3072, 0:15], z_up[:, T - 15:T], s_s)
        sc1 = P.newsem("ex1c", bg=True)
        g1 = P.coll(PAIR, ex1sp[0][:, :], ex1rp[0][:, :], sc1, waits=[d1, d1a, d1b, d2, d3, d4])
        g1 = P.coll(PAIR, ex1sp[1][:, :], ex1rp[1][:, :], sc1, waits=[g1])
        g1 = P.coll(PAIR, ex1sp[2][:, :], ex1rp[2][:, :], sc1, waits=[g1])
        g2 = P.coll(PAIR, ex1hs[:, :], ex1hr[:, :], sc1, waits=[g1])
        ex_sem[("ex1", l)] = g2
        A.release(m)

    def phase_convpool(l):
        m = A.mark()
        vv = vecs[l]
        ex = ex_sem[("ex1", l)]
        hflag = cst[:, C_H:C_H + 1]
        bufs = [[A.alloc([T + 16], F32) for _ in range(4)] for _ in range(2)]
        s_ld = [P.newsem(f"pp_ld{i}") for i in range(2)]
        s_st = [P.newsem(f"pp_st{i}") for i in range(2)]
        free = [None, None]
        for fc in range(8):
            b = fc % 2
            bg, cg, v, tmp = bufs[b]
            r = slice(fc * 128, (fc + 1) * 128)
            w8 = [free[b], ex]
            P.dma("sync", bg[:, 0:T], z_conv[r, :], s_ld[b], waits=w8)
            P.dma("sync", cg[:, 2:T + 2], z_conv[1024 + fc * 128:1024 + (fc + 1) * 128, :], s_ld[b], waits=w8)
            P.dma("sync", v[:, 2:T + 2], z_conv[2048 + fc * 128:2048 + (fc + 1) * 128, :], s_ld[b], waits=w8)
            P.dma("sync", cg[:, 0:2], ex1hr[fc * 128:(fc + 1) * 128, 0:2], s_ld[b], waits=w8)
            ld = P.dma("sync", v[:, 0:2], ex1hr[1024 + fc * 128:1024 + (fc + 1) * 128, 0:2], s_ld[b], waits=w8)
            o1 = P.op("vector", lambda e, cg=cg, v=v: e.tensor_tensor(cg[:, 0:T + 2], cg[:, 0:T + 2], v[:, 0:T + 2], ALU.mult), waits=[ld])
            o2 = P.op("vector", lambda e, cg=cg: e.tensor_scalar(cg[:, 0:2], cg[:, 0:2], hflag, None, ALU.mult), waits=[o1])
            w0 = vv[:, V_CA + 0 * 8 + fc:V_CA + 0 * 8 + fc + 1]
            w1 = vv[:, V_CA + 1 * 8 + fc:V_CA + 1 * 8 + fc + 1]
            w2 = vv[:, V_CA + 2 * 8 + fc:V_CA + 2 * 8 + fc + 1]
            o3 = P.op("vector", lambda e, cg=cg, tmp=tmp, w0=w0: e.tensor_scalar(tmp[:, 0:T], cg[:, 0:T], w0, None, ALU.mult), waits=[o2])
            o4 = P.op("vector", lambda e, cg=cg, tmp=tmp, w1=w1: e.scalar_tensor_tensor(tmp[:, 0:T], cg[:, 1:T + 1], w1, tmp[:, 0:T], ALU.mult, ALU.add), waits=[o3])
            o5 = P.op("vector", lambda e, cg=cg, tmp=tmp, w2=w2: e.scalar_tensor_tensor(tmp[:, 0:T], cg[:, 2:T + 2], w2, tmp[:, 0:T], ALU.mult, ALU.add), waits=[o4])
            yb = v.bitcast(BF16)[:, 0:T]
            o6 = P.op("vector", lambda e, bg=bg, tmp=tmp, yb=yb: e.tensor_tensor(yb, tmp[:, 0:T], bg[:, 0:T], ALU.mult), waits=[o5])
            free[b] = P.dma("sync", yT[r, :], yb, s_st[b], waits=[o6])
        dT = A.alloc([8, T], BF16)
        pw = A.alloc([4, 2, 256], BF16)
        s_pw = P.newsem("m0")
        pwt = P.dma("gpsimd", pw, pool_w_in[l].rearrange("g (kc k) n -> k g kc n", k=128), s_pw)
        free = [None, None]
        dlast = None
        for fc in range(8):
            b = fc % 2
            u, s_a, s_b, tmp = bufs[b]
            g = fc // 2
            wnd = 2 << g
            w8 = [free[b], ex]
            z0 = P.op("gpsimd", lambda e, u=u: e.memset(u[:, 0:1], 0.0), waits=w8)
            P.dma("sync", u[:, 16:T + 16], z_up[fc * 128:(fc + 1) * 128, :], s_ld[b], waits=w8)
            ld = P.dma("sync", u[:, 1:16], ex1hr[2048 + fc * 128:2048 + (fc + 1) * 128, 0:15], s_ld[b], waits=w8)
            o = P.op("vector", lambda e, u=u: e.tensor_scalar(u[:, 1:16], u[:, 1:16], hflag, None, ALU.mult), waits=[ld, z0])
            cur = u
            alt = [s_a, s_b]
            ai = 0
            k = 1
            while k < wnd:
                dst = alt[ai]
                ai ^= 1
                lo = 2 * k - 1
                o = P.op("vector", lambda e, cur=cur, dst=dst, lo=lo, k=k: e.tensor_tensor(
                    dst[:, lo:T + 16], cur[:, lo:T + 16], cur[:, lo - k:T + 16 - k], ALU.add), waits=[o])
                cur = dst
                k *= 2
            o = P.op("vector", lambda e, cur=cur, tmp=tmp, wnd=wnd: e.tensor_scalar(tmp[:, 0:T], cur[:, 16:T + 16], 1.0 / wnd, None, ALU.mult), waits=[o])
            o = P.op("vector", lambda e, tmp=tmp, g=g: e.tensor_tensor(tmp[:, 0:16], tmp[:, 0:16], cst[:, C_PC + g * 16:C_PC + (g + 1) * 16], ALU.mult), waits=[o])
            dlast = P.op("vector", lambda e, tmp=tmp, u=u, fc=fc: e.tensor_tensor(dT[:, fc, :], tmp[:, 0:T], u[:, 16:T + 16], ALU.subtract), waits=[o])
            free[b] = dlast
        st = Stage(4, "A")
        for tc in range(NTC):
            for g in range(4):
                for mo in range(2):
                    pb = psrr[0] % 4
                    psrr[0] += 1
                    mm = None
                    for kc in range(2):
                        mm = P.op("tensor", lambda e, g=g, mo=mo, kc=kc, pb=pb, tc=tc: e.matmul(
                            ps[pb][:, :], pw[:, g, kc, mo * 128:(mo + 1) * 128], dT[:, g * 2 + kc, tcols(tc)],
                            start=(kc == 0), stop=(kc == 1)), waits=[pwt, dlast, ps_free[pb]] if kc == 0 else [], sig=(kc == 1))
                    k = st.get()
                    o = bfview(st.buf[k])
                    fcn = g * 2 + mo
                    c = P.op("vector", lambda e, o=o, pb=pb, fcn=fcn: e.tensor_scalar(o, ps[pb][:, :], vv[:, V_PS + fcn:V_PS + fcn + 1], None, ALU.mult),
                             waits=[mm, st.last[k]])
                    ps_free[pb] = c
                    st.last[k] = P.dma("sync", yT[3072 + fcn * 128:3072 + (fcn + 1) * 128, tcols(tc)], o, st.sem[k], waits=[c])
        A.release(m)
    rfree = {}
    pfree = {}
    afree = [None, None]
    mfree = [None]
    ofree = [None]
    def phase_attn(l):
        m = A.mark()
        ex = ex_sem[("ex1", l)]
        NK = 2 * T
        kiT = A.alloc([NK], BF16, parts=64)
        ckvT = A.alloc([2, NK], BF16)
        ckvtok = A.alloc([NK // 128, 256], BF16)
        wuk = A.alloc([NH, 256], BF16)
        wuv = A.alloc([NH, 2, 128], BF16)
        acc = A.alloc([NK], F32)
        work = A.alloc([NK], F32)
        maskT = A.alloc([NK // 128, 128], BF16)
        qlat = A.alloc([2, NH, 128], BF16)
        qTb = [A.alloc([NH, 128], BF16) for _ in range(2)]
        qib = [A.alloc([IDXH, 128], BF16, parts=64) for _ in range(2)]
        wiTb = [A.alloc([128], F32, parts=32) for _ in range(2)]
        wtok = A.alloc([32], F32)
        m8 = A.alloc([256], F32)
        thr = A.alloc([1], F32)
        Rb = [A.alloc([512], F32) for _ in range(3)]
        pTb = [A.alloc([512], BF16) for _ in range(3)]
        rden = A.alloc([512], F32)
        onb = A.alloc([2, 512], BF16)
        ybst = [A.alloc([512], BF16) for _ in range(2)]
        s_k = P.newsem("m0")
        s_q = [P.newsem(f"pp_ld{i}") for i in range(2)]
        s_y = [P.newsem(f"pp_st{i}") for i in range(2)]
        lds = []
        lds.append(P.dma("sync", kiT[:, 0:T], ex1rp[2][0:64, :], s_k, waits=[ex]))
        lds.append(P.dma("sync", kiT[:, T:NK], z_ki[:, :], s_k))
        lds.append(P.dma("sync", ckvT[:, 0, 0:T], ex1rp[0][0:128, :], s_k, waits=[ex]))
        lds.append(P.dma("sync", ckvT[:, 1, 0:T], ex1rp[1][0:128, :], s_k, waits=[ex]))
        lds.append(P.dma("sync", ckvT[:, :, T:NK], ex1s[0:256, :].rearrange("(c p) t -> p c t", p=128), s_k))
        lds.append(P.dma("sync", wuk, wukb[l].rearrange("h d c -> d h c"), s_k))
        lds.append(P.dma("sync", wuv, wuvb[l].rearrange("h (cc c) d -> c h cc d", c=128), s_k))
        kld = lds[-1]
        lastck = None
        for j2 in range(NK // 256):
            pb = 4 + j2 % 2
            mm = None
            for jj in range(2):
                j = j2 * 2 + jj
                for cc in range(2):
                    mm = P.op("tensor", lambda e, j=j, jj=jj, cc=cc, pb=pb: e.matmul(
                        ps[pb][:, jj * 256 + cc * 128:jj * 256 + (cc + 1) * 128], ckvT[:, cc, j * 128:(j + 1) * 128], identb,
                        start=True, stop=True), waits=[lds, ps_free[pb]] if (jj == 0 and cc == 0) else [], sig=(jj == 1 and cc == 1))
            lastck = copy_op(evac_engine(), ckvtok[:, j2 * 2:j2 * 2 + 2, :], ps[pb][:, :].rearrange("p (a b) -> p a b", b=256), [mm])
            ps_free[pb] = lastck
        P.fence()

        qfree = [None, None]
        yfree = [None, None]
        rrB = [0]
        rrE = [0]
        sc_scale = float(128 ** -0.5)
        S = {}

        def stage_A1(i):
            b = i % 2
            t0 = i * 128
            qi_i, wiT, qT = qib[b], wiTb[b], qTb[b]
            P.dma("sync", qi_i, z_qi[:, t0:t0 + 128].rearrange("(h d) t -> d h t", d=64), s_q[b], waits=[qfree[b]])
            P.dma("sync", wiT, z_wi[:, t0:t0 + 128], s_q[b], waits=[qfree[b]])
            qld = P.dma("sync", qT, z_q[:, t0:t0 + 128].rearrange("(h d) t -> d h t", d=128), s_q[b], waits=[qfree[b]])
            mm = P.op("tensor", lambda e, wiT=wiT: e.matmul(ps[4][:, 0:32], wiT, identf[0:32, 0:32], start=True, stop=True),
                      waits=[qld, ps_free[4]])
            wt = P.op("vector", lambda e: e.tensor_copy(wtok, ps[4][:, 0:32]), waits=[mm, S.get("wtok_free")])
            ps_free[4] = wt
            S[i] = dict(qld=qld, wt=wt)

        def stage_A2(i):
            b = i % 2
            qT = qTb[b]
            qld = S[i]["qld"]
            ql_last = []
            for cc in range(2):
                for h4 in range(4):
                    pb = 4 + rrB[0] % 2
                    rrB[0] += 1
                    mm = None
                    for hl in range(4):
                        h = h4 * 4 + hl
                        mm = P.op("tensor", lambda e, h=h, hl=hl, cc=cc, pb=pb, qT=qT: e.matmul(
                            ps[pb][:, hl * 128:(hl + 1) * 128], wuk[:, h, cc * 128:(cc + 1) * 128], qT[:, h, :], start=True, stop=True),
                            waits=[qld, ps_free[pb], S.get("qlat_free")] if hl == 0 else [], sig=(hl == 3))
                    c = copy_op(evac_engine(), qlat[:, cc, h4 * 4:(h4 + 1) * 4, :], ps[pb][:, :].rearrange("p (a b) -> p a b", b=128),
                                [mm, S.get("qlat_free")])
                    ps_free[pb] = c
                    ql_last.append(c)
            S[i]["ql_last"] = ql_last
            qfree[b] = [qfree[b]] + ql_last

        def gen_B(i):
            b = i % 2
            qi_i = qib[b]
            W = T + (i + 1) * 128
            qld, wt = S[i]["qld"], S[i]["wt"]
            npieces = (W + 511) // 512
            acc_tok = [None] * npieces
            for kp in range(npieces):
                w = min(512, W - kp * 512)
                for h in range(IDXH):
                    pb = 4 + rrB[0] % 2
                    rrB[0] += 1
                    mm = P.op("tensor", lambda e, h=h, kp=kp, w=w, pb=pb, qi_i=qi_i: e.matmul(
                        ps[pb][:, 0:w], qi_i[:, h, :], kiT[:, kp * 512:kp * 512 + w], start=True, stop=True),
                        waits=[qld, ps_free[pb]])
                    R = Rb[rrB[0] % 3]
                    rt = P.op("scalar", lambda e, R=R, w=w, pb=pb: e.activation(out=R[:, 0:w], in_=ps[pb][:, 0:w], func=AF.Relu),
                              waits=[mm, rfree.get(id(R))])
                    ps_free[pb] = rt
                    a_sl = acc[:, kp * 512:kp * 512 + w]
                    if h == 0:
                        at = P.op("vector", lambda e, R=R, w=w, a_sl=a_sl: e.tensor_scalar(a_sl, R[:, 0:w], wtok[:, 0:1], None, ALU.mult),
                                  waits=[rt, wt, afree[0]])
                    else:
                        at = P.op("vector", lambda e, R=R, w=w, a_sl=a_sl, h=h: e.scalar_tensor_tensor(
                            a_sl, R[:, 0:w], wtok[:, h:h + 1], a_sl, ALU.mult, ALU.add), waits=[rt, acc_tok[kp]])
                    acc_tok[kp] = at
                    rfree[id(R)] = at
                    yield
            qfree[b] = acc_tok[-1]
            S["wtok_free"] = acc_tok
            S[i]["acc_tok"] = acc_tok

        def gen_C(i):
            W = T + (i + 1) * 128
            acc_tok = S[i]["acc_tok"]
            o = P.op("vector", lambda e: e.tensor_scalar(acc[:, 0:T], acc[:, 0:T], cst[:, C_NEGB:C_NEGB + 1], None, ALU.add), waits=[acc_tok])
            o = P.op("vector", lambda e, W=W: e.tensor_tensor(acc[:, W - 128:W], acc[:, W - 128:W], tri, ALU.add), waits=[o, acc_tok])
            yield
            cur = acc
            for r in range(TOPK // 8):
                o = P.op("vector", lambda e, cur=cur, r=r, W=W: e.max(out=m8[:, r * 8:(r + 1) * 8], in_=cur[:, 0:W]), waits=[o, afree[1]])
                if r < TOPK // 8 - 1:
                    o = P.op("vector", lambda e, cur=cur, r=r, W=W: e.match_replace(
                        out=work[:, 0:W], in_to_replace=m8[:, r * 8:(r + 1) * 8], in_values=cur[:, 0:W], imm_value=-3.0e38), waits=[o])
                    cur = work
                yield
            o = P.op("vector", lambda e: e.tensor_scalar(thr, m8[:, TOPK - 1:TOPK], -1.0e29, None, ALU.max), waits=[o])
            mk = P.op("vector", lambda e, W=W: e.tensor_scalar(work[:, 0:W], acc[:, 0:W], thr[:, 0:1], None, ALU.is_ge), waits=[o])
            afree[0] = mk
            S[i]["mk"] = mk
            yield

        def stage_D(i):
            W = T + (i + 1) * 128
            NJ = W // 128
            mk = S[i]["mk"]
            mt_last = None
            for j4 in range((NJ + 3) // 4):
                pb = 4 + rrB[0] % 2
                rrB[0] += 1
                nj = min(4, NJ - j4 * 4)
                mm = None
                for jj in range(nj):
                    j = j4 * 4 + jj
                    mm = P.op("tensor", lambda e, j=j, jj=jj, pb=pb: e.matmul(
                        ps[pb][:, jj * 128:(jj + 1) * 128], work[:, j * 128:(j + 1) * 128], identf, start=True, stop=True),
                        waits=[mk, ps_free[pb]] if jj == 0 else [], sig=(jj == nj - 1))
                mt_last = copy_op(evac_engine(), maskT[:, j4 * 4:j4 * 4 + nj, :],
                                  ps[pb][:, 0:nj * 128].rearrange("p (a b) -> p a b", b=128), [mm, mfree[0]])
                ps_free[pb] = mt_last
            afree[1] = mt_last
            S[i]["mt_last"] = mt_last

        def gen_E(i):
            t0 = i * 128
            W = T + (i + 1) * 128
            NJ = W // 128
            ql_last = S[i]["ql_last"]
            mt_last = S[i]["mt_last"]
            pv_last = None
            sc_last = None
            for hg in range(4):
                for j in range(NJ):
                    pbs = rrE[0] % 2
                    rrE[0] += 1
                    mm = None
                    for cc in range(2):
                        mm = P.op("tensor", lambda e, j=j, cc=cc, pbs=pbs, hg=hg: e.matmul(
                            ps[pbs][:, :], ckvT[:, cc, j * 128:(j + 1) * 128], qlat[:, cc, hg * 4:(hg + 1) * 4, :],
                            start=(cc == 0), stop=(cc == 1)), waits=[ql_last, ps_free[pbs]] if cc == 0 else [], sig=(cc == 1))
                    sc_last = mm
                    pT = pTb[rrE[0] % 3]
                    et = P.op("scalar", lambda e, pT=pT, pbs=pbs: e.activation(out=pT, in_=ps[pbs][:, :], func=AF.Exp, scale=sc_scale),
                              waits=[mm, pfree.get(id(pT))])
                    ps_free[pbs] = et
                    mt = P.op("vector", lambda e, pT=pT, j=j: e.tensor_tensor(
                        pT.rearrange("p (a b) -> p a b", b=128), pT.rearrange("p (a b) -> p a b", b=128),
                        maskT[:, j:j + 1, :].broadcast_to([128, 4, 128]), ALU.mult), waits=[et, mt_last])
                    for cc in range(2):
                        P.op("tensor", lambda e, j=j, cc=cc, pT=pT, NJ=NJ: e.matmul(
                            ps[2 + cc][:, :], ckvtok[:, j, cc * 128:(cc + 1) * 128], pT, start=(j == 0), stop=(j == NJ - 1)),
                            waits=[mt, ps_free[2 + cc], lastck] if j == 0 else [mt], sig=False)
                    pv_last = P.op("tensor", lambda e, j=j, pT=pT, NJ=NJ: e.matmul(
                        ps[6][:, :], onesb, pT, start=(j == 0), stop=(j == NJ - 1)), waits=[ps_free[6]] if j == 0 else [])
                    pfree[id(pT)] = pv_last
                    yield
                r1 = P.op("vector", lambda e: e.reciprocal(rden, ps[6][:, :]), waits=[pv_last, ofree[0]])
                ps_free[6] = r1
                n_last = None
                for cc in range(2):
                    n_last = P.op("vector", lambda e, cc=cc: e.tensor_tensor(onb[:, cc, :], ps[2 + cc][:, :], rden, ALU.mult), waits=[r1])
                    ps_free[2 + cc] = n_last
                mm = None
                for hl in range(4):
                    h = hg * 4 + hl
                    for cc in range(2):
                        mm = P.op("tensor", lambda e, h=h, hl=hl, cc=cc: e.matmul(
                            ps[7][:, hl * 128:(hl + 1) * 128], wuv[:, h, cc, :], onb[:, cc, hl * 128:(hl + 1) * 128],
                            start=(cc == 0), stop=(cc == 1)), waits=[n_last, ps_free[7]] if (hl == 0 and cc == 0) else [],
                            sig=(hl == 3 and cc == 1))
                ofree[0] = mm
                yb = ybst[(i * 4 + hg) % 2]
                c = copy_op(evac_engine(), yb, ps[7][:, :], [mm, yfree[(i * 4 + hg) % 2]])
                ps_free[7] = c
                yfree[(i * 4 + hg) % 2] = P.dma(
                    "sync", yT[1024 + hg * 512:1024 + (hg + 1) * 512, t0:t0 + 128].rearrange("(h d) t -> d h t", d=128),
                    yb.rearrange("p (h t) -> p h t", t=128), s_y[(i * 4 + hg) % 2], waits=[c])
                yield
            mfree[0] = pv_last
            S["qlat_free"] = sc_last

        def drain(g):
            for _ in g:
                pass

        def chain2(g1f, g2f):
            for _ in g1f():
                yield
            for _ in g2f():
                yield

        stage_A1(0)
        drain(gen_B(0))
        drain(gen_C(0))
        stage_D(0)
        stage_A2(0)
        for i in range(NQ):
            if i + 1 < NQ:
                stage_A1(i + 1)
                W1 = T + (i + 2) * 128
                nBC = 32 * ((W1 + 511) // 512) + 34
                nE = 4 * ((T + (i + 1) * 128) // 128) + 4
                gE = gen_E(i)
                gBC = chain2(lambda: gen_B(i + 1), lambda: gen_C(i + 1))
                accn = 0.0
                bc_done = False
                for _ in gE:
                    accn += nBC / nE
                    while accn >= 1.0 and not bc_done:
                        accn -= 1.0
                        try:
                            next(gBC)
                        except StopIteration:
                            bc_done = True
                if not bc_done:
                    drain(gBC)
                stage_D(i + 1)
                stage_A2(i + 1)
            else:
                drain(gen_E(i))

        A.release(m)

    class Ring:
        def __init__(self, n, name, dtype=F32, width=512):
            self.n = n
            self.buf = [A.alloc([width], dtype) for _ in range(n)]
            self.sem = [P.newsem(f"rg_{name}_{i}") for i in range(n)]
            self.free = [None] * n
            self.i = 0
            self.q = []

        def fetch(self, src, eng="sync", waits=None):
            k = self.i % self.n
            self.i += 1
            tok = P.dma(eng, self.buf[k], src, self.sem[k], waits=[self.free[k], waits])
            self.q.append((k, tok))

        def pop(self):
            return self.q.pop(0)

    lnfree = [None]
    mgfree = {}

    def ln_stats(vbuf, sqbuf, vtok, n, last):
        sq = P.op("scalar", lambda e: e.activation(out=sqbuf, in_=vbuf, func=AF.Square), waits=[vtok, lnfree[0]])
        m1 = P.op("tensor", lambda e: e.matmul(ps[6][:, :], onesf, vbuf, start=(n == 0), stop=last),
                  waits=[vtok, ps_free[6]] if n == 0 else [vtok], sig=True)
        m2 = P.op("tensor", lambda e: e.matmul(ps[7][:, :], onesf, sqbuf, start=(n == 0), stop=last),
                  waits=[sq, ps_free[7]] if n == 0 else [sq], sig=True)
        lnfree[0] = m2
        return m1, m2


    def ln_finish(l, tc, gcol, bcol, dst32, dstb, stat_tok):
        m = A.mark()
        vv = vecs[l]
        mean = A.alloc([512], F32)
        rstd = A.alloc([512], F32)
        nmr = A.alloc([512], F32)
        t1 = P.op("vector", lambda e: e.tensor_scalar(mean, ps[6][:, :], 1.0 / D, None, ALU.mult), waits=[stat_tok])
        t2 = P.op("vector", lambda e: e.tensor_tensor(nmr, mean, mean, ALU.mult), waits=[t1])
        t3 = P.op("vector", lambda e: e.scalar_tensor_tensor(rstd, ps[7][:, :], 1.0 / D, nmr, ALU.mult, ALU.subtract), waits=[t2])
        ps_free[6] = t3
        ps_free[7] = t3
        epsl = A.alloc([1], F32)
        te = P.op("vector", lambda e: e.memset(epsl, LN_EPS))
        t4a = P.op("scalar", lambda e: e.activation(out=rstd, in_=rstd, func=AF.Sqrt, bias=epsl, scale=1.0), waits=[t3, te])
        t4 = P.op("vector", lambda e: e.reciprocal(rstd, rstd), waits=[t4a])
        t5 = P.op("vector", lambda e: e.scalar_tensor_tensor(nmr, mean, -1.0, rstd, ALU.mult, ALU.mult), waits=[t4])
        rg = Ring(4, "B")
        st = Stage(6, "B")
        P.fence()
        for n in range(min(3, KC)):
            rg.fetch(vT[n * 128:(n + 1) * 128, tcols(tc)])
        for n in range(KC):
            k, ld = rg.pop()
            vb = rg.buf[k]
            a1 = P.op("vector", lambda e, vb=vb: e.tensor_tensor(vb, vb, rstd, ALU.mult), waits=[ld, t5])
            a2 = P.op("vector", lambda e, vb=vb: e.tensor_tensor(vb, vb, nmr, ALU.add), waits=[a1])
            k1 = st.get()
            o32 = st.buf[k1]
            c1 = P.op("scalar", lambda e, vb=vb, o32=o32, n=n: e.activation(
                out=o32, in_=vb, func=AF.Identity, bias=vv[:, bcol + n:bcol + n + 1], scale=vv[:, gcol + n:gcol + n + 1]),
                waits=[a2, st.last[k1]])
            rg.free[k] = c1
            st.last[k1] = P.dma("sync", dst32[n * 128:(n + 1) * 128, tcols(tc)], o32, st.sem[k1], waits=[c1])
            k2 = st.get()
            ob = bfview(st.buf[k2])
            c2 = P.op("vector", lambda e, ob=ob, o32=o32: e.tensor_copy(ob, o32), waits=[c1, st.last[k2]])
            st.last[k2] = P.dma("sync", dstb[n * 128:(n + 1) * 128, tcols(tc)], ob, st.sem[k2], waits=[c2])
            st.last[k1] = [st.last[k1], c2]
            if n + 3 < KC:
                rg.fetch(vT[(n + 3) * 128:(n + 4) * 128, tcols(tc)])
        A.release(m)

    def make_res_epi(l, tc, res_src, a_scale, st, rg, sqb, do_stats, stat_out):
        def epi(pst, w, tag, mm):
            n = tag
            k, ld = rg.pop()
            rb = rg.buf[k]
            ks = st.get()
            vb = st.buf[ks]
            c = P.op("vector", lambda e: e.scalar_tensor_tensor(vb, rb, float(a_scale), pst[:, :], ALU.mult, ALU.add),
                     waits=[mm, ld, st.last[ks]])
            rg.free[k] = c
            d = P.dma("sync", vT[n * 128:(n + 1) * 128, tcols(tc)], vb, st.sem[ks], waits=[c])
            st.last[ks] = d
            if do_stats:
                sq = sqb[n % 2]
                m1, m2 = ln_stats(vb, sq, c, n, n == KC - 1)
                st.last[ks] = [d, m1]
                stat_out[0] = m2
            if n + 3 < KC:
                rg.fetch(res_src[(n + 3) * 128:(n + 4) * 128, tcols(tc)])
            return c
        return epi

    def phase_merge(l, tc):
        m = A.mark()
        Am = A.alloc([KC, 512], BF16)
        ws = WStream(2, KC * 512)
        m2 = A.mark()
        Ay = A.alloc([KC, 512], BF16)
        s_a = P.newsem("a_ld")
        a_tok = load_A(Ay, yT[:, tcols(tc)], KC, s_a)
        grg = Ring(3, "G", dtype=BF16, width=3 * 512)
        tmpa = [A.alloc([512], F32) for _ in range(2)]
        tmpb = [A.alloc([512], F32) for _ in range(2)]
        zg = z_g[:, tcols(tc)].rearrange("(br c p) t -> p br c t", br=3, p=128)
        ready = [wready[("w_br_a", l)], wready[("w_br_b", l)], wready[("w_br_c", l)]]
        kgrp = [(0, 8, "w_br_a"), (8, 16, "w_br_b"), (24, 8, "w_br_c")]

        def load_block(bi):
            k = ws.i % ws.n
            ws.i += 1
            slot = ws.slots[k][:, 0:KC * 512].rearrange("p (c n) -> p c n", n=512)
            tok = None
            for (ka, kn, nme) in kgrp:
                hc = wmeta[nme][0] // 256
                for a in range(0, kn, min(8, hc)):
                    bnd = min(kn, a + min(8, hc))
                    src = wrows(nme, l, a * 128, bnd * 128)[:, bi * 512:(bi + 1) * 512].rearrange("(c p) n -> p c n", p=128)
                    tok = P.dma("scalar", slot[:, ka + a:ka + bnd, :], src, ws.sems[k],
                                waits=[ws.free_tok[k], ready])
            return k, slot, tok
        nb = D // 512
        loaded = {}
        for bi in range(min(1, nb)):
            loaded[bi] = load_block(bi)
        for n in range(2):
            grg.fetch(zg[:, :, n, :], waits=None)
        grp_i = 0
        am_last = None
        for bi in range(nb):
            k, slot, wtok = loaded.pop(bi)
            last_mm = None
            for j in range(4):
                n = bi * 4 + j
                banks = (0, 1, 2) if grp_i % 2 == 0 else (3, 4, 5)
                grp_i += 1
                mms = []
                for gi, (ka, kn, nme) in enumerate(kgrp):
                    pb = banks[gi]
                    mm = None
                    for kc in range(kn):
                        mm = P.op("tensor", lambda e, pb=pb, kc=kc, ka=ka, kn=kn, j=j, slot=slot: e.matmul(
                            ps[pb][:, :], slot[:, ka + kc, j * 128:(j + 1) * 128], Ay[:, ka + kc, :],
                            start=(kc == 0), stop=(kc == kn - 1)),
                            waits=[wtok, a_tok, ps_free[pb]] if kc == 0 else [], sig=(kc == kn - 1))
                    mms.append(mm)
                last_mm = mms[-1]
                gk, gld = grg.pop()
                gb = grg.buf[gk].rearrange("p (a b) -> p a b", b=512)
                ta, tb = tmpa[n % 2], tmpb[n % 2]
                o1 = P.op("vector", lambda e, ta=ta, gb=gb, pb=banks[0]: e.tensor_tensor(ta, ps[pb][:, :], gb[:, 0, :], ALU.mult), waits=[mms[0], gld, mgfree.get(id(ta))])
                ps_free[banks[0]] = o1
                o2 = P.op("vector", lambda e, tb=tb, gb=gb, pb=banks[1]: e.tensor_tensor(tb, ps[pb][:, :], gb[:, 1, :], ALU.mult), waits=[mms[1], gld, mgfree.get(id(tb))])
                ps_free[banks[1]] = o2
                o3 = P.op("vector", lambda e, ta=ta, tb=tb: e.tensor_tensor(ta, ta, tb, ALU.add), waits=[o1, o2])
                o4 = P.op("vector", lambda e, tb=tb, gb=gb, pb=banks[2]: e.tensor_tensor(tb, ps[pb][:, :], gb[:, 2, :], ALU.mult), waits=[mms[2], o3])
                ps_free[banks[2]] = o4
                grg.free[gk] = o4
                o5 = P.op("vector", lambda e, ta=ta, tb=tb, n=n: e.tensor_tensor(Am[:, n, :], ta, tb, ALU.add), waits=[o4])
                mgfree[id(ta)] = o5
                mgfree[id(tb)] = o5
                am_last = o5
                if n + 2 < KC:
                    grg.fetch(zg[:, :, n + 2, :])
            ws.free_tok[k] = last_mm
            if bi + 1 < nb:
                loaded[bi + 1] = load_block(bi + 1)
        P.fence()
        A.release(m2)
        st = Stage(4, "A")
        rg = Ring(4, "A")
        sqb = [A.alloc([512], F32) for _ in range(2)]
        for n in range(3):
            rg.fetch(xT32[n * 128:(n + 1) * 128, tcols(tc)])
        stat_out = [None]
        epi = make_res_epi(l, tc, xT32, ALPHA, st, rg, sqb, True, stat_out)
        ch = [(n * 128, 128, n) for n in range(KC)]
        gemm(wfull[("w_o", l)], wready[("w_o", l)], Am, am_last, 0, KC, mkblocks(ch), ws, epi)
        ln_finish(l, tc, V_L1G, V_L1B, x1T32, x1Tb, stat_out[0])
        A.release(m)

    def phase_ex2(l):
        s = P.newsem("m0")
        d = P.dma("sync", ex2s[:, :], x1Tb[:, T - 16:T], s)
        sc = P.newsem("ex2c", bg=True)
        ex_sem[("ex2", l)] = P.coll(PAIR, ex2s[:, :], ex2r[:, :], sc, waits=[d])

    def phase_ffn_up(l, tc):
        m = A.mark()
        vv = vecs[l]
        Ax = A.alloc([KC, 512], BF16)
        s_a = P.newsem("a_ld")
        a_tok = load_A(Ax, x1Tb[:, tcols(tc)], KC, s_a)
        ws = WStream(3, KC * 512)
        st = Stage(4, "A")
        gbuf = [A.alloc([520], F32) for _ in range(2)]
        cbuf = [A.alloc([512], F32) for _ in range(2)]
        sbuf = [A.alloc([512], F32) for _ in range(8)]
        hflag = cst[:, C_H:C_H + 1]
        extra = None
        if tc == 0:
            xh = A.alloc([KC, 16], BF16)
            s_h = P.newsem("a_ldh")
            htok = P.dma("sync", xh, ex2r[0:D, :].rearrange("(c p) t -> p c t", p=128), s_h, waits=[ex_sem[("ex2", l)]])
            extra = (xh, htok)
        gfree = [None, None]
        cfree = [None, None]
        sfree = [None] * 8
        stok = {}

        def epi(pst, w, tag, mm, xtok=None):
            kind, n = tag
            if kind == "gt":
                gb = gbuf[n % 2]
                cb = cbuf[n % 2]
                sb = sbuf[n % 8]
                c0 = P.op("scalar", lambda e: e.activation(out=gb[:, 2:514], in_=pst[:, :], func=AF.Identity), waits=[mm, gfree[n % 2]])
                if tc == 0:
                    h0 = P.op("vector", lambda e: e.tensor_scalar(gb[:, 0:2], ps[4][:, 14:16], hflag, None, ALU.mult), waits=[xtok, gfree[n % 2]])
                    ps_free[4] = h0
                else:
                    h0 = P.op("vector", lambda e: e.tensor_copy(gb[:, 0:2], ffhalo[:, n, :]), waits=[gfree[n % 2]])
                h1 = P.op("vector", lambda e: e.tensor_copy(ffhalo[:, n, :], gb[:, 512:514]), waits=[c0, h0])
                w0 = vv[:, V_CF + 0 * NFF + n:V_CF + 0 * NFF + n + 1]
                w1 = vv[:, V_CF + 1 * NFF + n:V_CF + 1 * NFF + n + 1]
                w2 = vv[:, V_CF + 2 * NFF + n:V_CF + 2 * NFF + n + 1]
                o1 = P.op("vector", lambda e: e.tensor_scalar(cb, gb[:, 0:512], w0, None, ALU.mult), waits=[c0, h0, cfree[n % 2]])
                o2 = P.op("vector", lambda e: e.scalar_tensor_tensor(cb, gb[:, 1:513], w1, cb, ALU.mult, ALU.add), waits=[o1])
                o3 = P.op("vector", lambda e: e.scalar_tensor_tensor(cb, gb[:, 2:514], w2, cb, ALU.mult, ALU.add), waits=[o2])
                gfree[n % 2] = [o3, h1]
                s1 = P.op("scalar", lambda e: e.activation(out=sb, in_=cb, func=AF.Silu), waits=[o3, sfree[n % 8]])
                cfree[n % 2] = s1
                stok[n] = s1
                return c0
            else:
                sb = sbuf[n % 8]
                k = st.get()
                ob = bfview(st.buf[k])
                c = P.op("vector", lambda e: e.tensor_tensor(ob, sb, pst[:, :], ALU.mult), waits=[mm, stok[n], st.last[k]])
                sfree[n % 8] = c
                st.last[k] = P.dma("sync", hT[n * 128:(n + 1) * 128, tcols(tc)], ob, st.sem[k], waits=[c])
                return c
        blocks = []
        for nb in range((NFF + 3) // 4):
            ns = list(range(nb * 4, min(NFF, nb * 4 + 4)))
            blocks.append([(n * 128, 128, ("gt", n)) for n in ns])
            blocks.append([(D_FF + n * 128, 128, ("up", n)) for n in ns])
        gemm(wfull[("w_up", l)], wready[("w_up", l)], Ax, a_tok, 0, KC, blocks, ws, epi, extra=extra)
        A.release(m)

    def phase_ffn_down(l, tc, dst32, dstb):
        m = A.mark()
        KH = NFF // 2
        Ah = A.alloc([KH, 512], BF16)
        ws = WStream(3, KH * 256)
        st = Stage(4, "A")
        rg = Ring(4, "A")
        sqb = [A.alloc([512], F32) for _ in range(2)]
        stat_out = [None]
        s_a = P.newsem("a_ld")
        ch = [(n * 128, 128, n) for n in range(KC)]
        for half in range(2):
            a_tok = load_A(Ah, hT[half * KH * 128:(half + 1) * KH * 128, tcols(tc)], KH, s_a)
            src = x1T32 if half == 0 else vT
            for n in range(3):
                rg.fetch(src[n * 128:(n + 1) * 128, tcols(tc)])
            epi = make_res_epi(l, tc, src, ALPHA if half == 0 else 1.0, st, rg, sqb, half == 1, stat_out)
            gemm(wfull[("w_down", l)], wready[("w_down", l)], Ah, a_tok, half * KH, KH, mkblocks(ch, per=2), ws, epi)
            P.fence()
        ln_finish(l, tc, V_L2G, V_L2B, dst32, dstb, stat_out[0])
        A.release(m)

    def phase_transpose_out():
        m = A.mark()
        xin = [A.alloc([KC, 128], F32) for _ in range(2)]
        yo = [A.alloc([D], F32) for _ in range(2)]
        s_l = [P.newsem(f"pp_ld{i}") for i in range(2)]
        s_s = [P.newsem(f"pp_st{i}") for i in range(2)]
        lfree = [None, None]
        sfree = [None, None]
        for ti in range(NQ):
            b = ti % 2
            ld = None
            for a in range(0, KC, 8):
                ld = P.dma("sync", xin[b][:, a:a + 8, :], xT32[a * 128:(a + 8) * 128, ti * 128:(ti + 1) * 128].rearrange("(c p) t -> p c t", p=128),
                           s_l[b], waits=[lfree[b]])
            cl = []
            mm = None
            for g4 in range(KC // 4):
                pb = g4 % 4
                for j in range(4):
                    c = g4 * 4 + j
                    mm = P.op("tensor", lambda e, c=c, j=j, pb=pb, b=b: e.matmul(
                        ps[pb][:, j * 128:(j + 1) * 128], xin[b][:, c, :], identf, start=True, stop=True),
                        waits=[ld, ps_free[pb]] if j == 0 else [], sig=(j == 3))
                cp = copy_op(evac_engine(), yo[b][:, g4 * 512:(g4 + 1) * 512], ps[pb][:, :], [mm, sfree[b]])
                ps_free[pb] = cp
                cl.append(cp)
            lfree[b] = mm
            sfree[b] = P.dma("sync", y_out[ti * 128:(ti + 1) * 128, :], yo[b], s_s[b], waits=cl)
        A.release(m)
    ALLW = ["w_in", "w_br_a", "w_br_b", "w_br_c", "w_o", "w_up", "w_down"]
    s_small = P.newsem("m3")
    import os
    for l in range(depth if not os.environ.get("NO_SMALL") else 0):
        P.dma("gpsimd", wukb[l], w_uk_in[l], s_small)
        P.dma("gpsimd", wuvb[l], w_uv_in[l], s_small)
        P.dma("gpsimd", poolwb[l], pool_w_in[l], s_small)
    prep_cast(ALLW, 0)
    prep_bar(ALLW, 0)

    stop = [False]

    def reached(name):
        if upto == name:
            stop[0] = True
        return stop[0]

    def run_all():
        if reached("init"):
            return
        phase_transpose_in()
        P.fence()
        if reached("tin"):
            return
        for l in range(depth):
            for tc in range(NTC):
                phase_inproj(l, tc)
                P.fence()
            if reached(f"inproj{l}"):
                return
            phase_ex1(l)
            P.fence()
            if l + 1 < depth:
                prep_cast(ALLW, l + 1)
            if reached(f"ex1{l}"):
                return
            phase_convpool(l)
            P.fence()
            if reached(f"convpool{l}"):
                return
            phase_attn(l)
            P.fence()
            if reached(f"attn{l}"):
                return
            for tc in range(NTC):
                phase_merge(l, tc)
                P.fence()
            if reached(f"merge{l}"):
                return
            phase_ex2(l)
            if l + 1 < depth:
                prep_bar(ALLW, l + 1)
            for tc in range(NTC):
                phase_ffn_up(l, tc)
                P.fence()
            if reached(f"ffnup{l}"):
                return
            for tc in range(NTC):
                phase_ffn_down(l, tc, xT32, xTb)
                P.fence()
            if reached(f"ffndown{l}"):
                return
        phase_transpose_out()

    P.fence()
    run_all()
    P.fence()
    scr = dict(xT32=xT32, xTb=xTb, x1T32=x1T32, x1Tb=x1Tb, vT=vT, z_conv=z_conv, z_q=z_q, z_ckv=z_ckv, z_qi=z_qi,
               z_ki=z_ki, z_wi=z_wi, z_up=z_up, z_g=z_g, yT=yT, hT=hT, ex1r=ex1r, ex1hr=ex1hr, ex2r=ex2r, ex1s=ex1s)
    s_dbg = P.newsem("m2")
    for nme in dbg:
        src = scr[nme]
        o = nc.dram_tensor("dbg_" + nme, list(src.shape), src.dtype, kind="ExternalOutput").ap()
        R = src.shape[0]
        for r0 in range(0, R, 1024):
            P.dma("sync", o[r0:min(R, r0 + 1024), :], src[r0:min(R, r0 + 1024), :], s_dbg)
    if stop[0]:
        P.dma("sync", y_out[0:128, :], x_in[0:128, :], s_dbg)
    final = P.alltoks()
    with nc.Block() as block:
        P.emit(block, final)
    es.close()
    return nc


def _pack_vecs(inp, depth):
    def fm(v):
        v = np.asarray(v, np.float32)
        return v.reshape(-1, 128).T
    out = np.zeros((depth, 128, NV), np.float32)
    for l in range(depth):
        o = out[l]
        for br in range(3):
            o[:, V_BG + br * 32:V_BG + (br + 1) * 32] = fm(inp["b_gate"][l, br])
        for tp in range(3):
            o[:, V_CA + tp * 8:V_CA + (tp + 1) * 8] = fm(inp["conv_a"][l, tp])
            o[:, V_CF + tp * NFF:V_CF + (tp + 1) * NFF] = fm(inp["conv_ffn_w"][l, tp])
        o[:, V_KVN:V_KVN + 2] = fm(inp["kv_norm"][l])
        o[:, V_PS:V_PS + 8] = fm(inp["pool_scale"][l])
        o[:, V_L1G:V_L1G + 32] = fm(inp["ln1_g"][l])
        o[:, V_L1B:V_L1B + 32] = fm(inp["ln1_b"][l])
        o[:, V_L2G:V_L2G + 32] = fm(inp["ln2_g"][l])
        o[:, V_L2B:V_L2B + 32] = fm(inp["ln2_b"][l])
    return out


def _cst(half):
    c = np.zeros((128, NCST), np.float32)
    c[:, C_H] = float(half)
    c[:, C_NEGB] = 0.0 if half else NEG
    for g, w in enumerate((2, 4, 8, 16)):
        for t in range(16):
            c[:, C_PC + g * 16 + t] = 1.0 if half else float(w) / float(min(t + 1, w))
    return c


def make_in_maps(inp, T, depth, need=None):
    inp = {k: np.asarray(v) for k, v in inp.items()}
    vecs = _pack_vecs(inp, depth)
    maps = []
    for c in range(NCORES):
        b, half = c // 2, c % 2
        m = {"x": np.ascontiguousarray(inp["x"][b, half * T:(half + 1) * T, :], dtype=np.float32)}
        for nme in ("w_in", "w_br_a", "w_br_b", "w_br_c", "w_o", "w_up", "w_down"):
            w = inp[nme][:depth]
            R2 = w.shape[1] // 2
            if need is not None and nme not in need:
                continue
            m[nme] = np.ascontiguousarray(w[:, half * R2:(half + 1) * R2, :], dtype=np.float32)
        m["w_uk"] = np.ascontiguousarray(inp["w_uk"][:depth], dtype=np.float32)
        m["w_uv"] = np.ascontiguousarray(inp["w_uv"][:depth], dtype=np.float32)
        m["pool_w"] = np.ascontiguousarray(inp["pool_w"][:depth], dtype=np.float32)
        m["vecs"] = vecs
        m["cst"] = _cst(half)
        maps.append(m)
    return maps


_NC_CACHE = {}


def kernel(**inputs):
    T = 2048
    if "full" not in _NC_CACHE:
        _NC_CACHE["full"] = build(T, DEPTH)
    nc = _NC_CACHE["full"]
    maps = make_in_maps(inputs, T, DEPTH)
    res = run_bass_kernel_spmd(nc, maps, core_ids=list(range(NCORES)))
    out = np.zeros((4, 2 * T, D), np.float32)
    for c in range(NCORES):
        b, half = c // 2, c % 2
        out[b, half * T:(half + 1) * T, :] = res.results[c]["y"]
    return out
```
